# Optimizing a Trainium2 kernel written in Bass

```python
import math
import jax, jax.numpy as jnp
from jax import lax
import numpy as np

D_MODEL = 1024
BATCH = 4
SEQ = 4096
DEPTH = 2

PLE_DIM = 256
D_FF = 2816
N_EVEN = (DEPTH + 1) // 2
N_ODD = DEPTH // 2
DN_ALPHA = (2.0 * DEPTH) ** 0.25
DN_BETA = (8.0 * DEPTH) ** -0.25
LN_EPS = 1e-5
SSD_INNER = D_MODEL
SSD_HEAD_DIM = 64
SSD_HEADS = SSD_INNER // SSD_HEAD_DIM
SSD_GROUPS = 4
SSD_STATE = 128
SSD_CONV = 4
SSD_CHUNK = 128
SSD_CONV_DIM = SSD_INNER + 2 * SSD_GROUPS * SSD_STATE
S5_WIDTH = D_MODEL
S5_GROUP = 16
S5_GROUPS = S5_WIDTH // S5_GROUP
S5_STATE = 64
HGRN_HEADS = 4
HGRN_KEY = 128
HGRN_VAL = 128
HGRN_WIDTH = HGRN_HEADS * HGRN_VAL
GLA_HEADS = 4
GLA_DK = 64
GLA_DV = 128
GLA_RANK = 16
GLA_TAU = 16.0
GLA_WIDTH = GLA_HEADS * GLA_DV
LIN_CHUNK = 64
AB_SPLITS = (SSD_INNER, SSD_CONV_DIM, SSD_HEADS, S5_WIDTH)
AB_IN = sum(AB_SPLITS)
AB_OUT = SSD_INNER + S5_WIDTH
CD_SPLITS = (HGRN_HEADS * HGRN_KEY, HGRN_HEADS * HGRN_KEY, HGRN_WIDTH, HGRN_WIDTH,
             GLA_HEADS * GLA_DK, GLA_HEADS * GLA_DK, GLA_WIDTH, GLA_RANK, GLA_WIDTH)
CD_IN = sum(CD_SPLITS)
CD_OUT = HGRN_WIDTH + GLA_WIDTH

kernel_name = "hybrid_ssd_s5_hgrn2_gla_macaron_deepnorm"


def split_cols(h, sizes):
    return jnp.split(h, np.cumsum(sizes)[:-1].tolist(), axis=-1)


def layer_norm(x, g, b):
    xf = x.astype(jnp.float32)
    mu = jnp.mean(xf, -1, keepdims=True)
    xc = xf - mu
    var = jnp.mean(xc * xc, -1, keepdims=True)
    return (xc * lax.rsqrt(var + LN_EPS) * g + b).astype(x.dtype)


def rms_norm(x, w):
    xf = x.astype(jnp.float32)
    return (xf * lax.rsqrt(jnp.mean(xf * xf, -1, keepdims=True) + LN_EPS) * w).astype(x.dtype)


def swiglu(x, w_gate, w_up, w_down):
    return (jax.nn.silu(x @ w_gate) * (x @ w_up)) @ w_down


def causal_depthwise_conv(x, w, b):
    k = w.shape[0]
    y = lax.conv_general_dilated(x, w[:, None, :], window_strides=(1,), padding=[(k - 1, 0)],
                                 dimension_numbers=('NWC', 'WIO', 'NWC'),
                                 feature_group_count=x.shape[-1])
    return y + b


def ssd_chunked(x, dt, a, bmat, cmat):
    b, l, h, p = x.shape
    g, n = bmat.shape[2:]
    r = h // g
    nc = l // SSD_CHUNK
    xdt = (x * dt[..., None]).reshape(b, nc, SSD_CHUNK, g, r, p)
    a_cum = jnp.cumsum((dt * a).reshape(b, nc, SSD_CHUNK, g, r), axis=2)
    bc = bmat.reshape(b, nc, SSD_CHUNK, g, n)
    cc = cmat.reshape(b, nc, SSD_CHUNK, g, n)
    causal = jnp.tril(jnp.ones((SSD_CHUNK, SSD_CHUNK), bool))[None, None, :, :, None, None]
    seg = a_cum[:, :, :, None] - a_cum[:, :, None, :]
    decay = jnp.exp(jnp.where(causal, seg, -jnp.inf))
    scores = jnp.einsum('bclgn,bcsgn->bclsg', cc, bc)
    y_diag = jnp.einsum('bclsgr,bcsgrp->bclgrp', scores[..., None] * decay, xdt)
    decay_to_end = jnp.exp(a_cum[:, :, -1:] - a_cum)
    states = jnp.einsum('bclgn,bclgrp->bcgrpn', bc, xdt * decay_to_end[..., None])
    chunk_decay = jnp.exp(a_cum[:, :, -1])

    def step(hstate, inp):
        dec, st = inp
        return dec[..., None, None] * hstate + st, hstate

    h0 = jnp.zeros((b, g, r, p, n), jnp.float32)
    _, prev = lax.scan(step, h0, (jnp.moveaxis(chunk_decay, 1, 0), jnp.moveaxis(states, 1, 0)))
    prev = jnp.moveaxis(prev, 0, 1)
    y_off = jnp.einsum('bclgn,bcgrpn->bclgrp', cc, prev) * jnp.exp(a_cum)[..., None]
    return (y_diag + y_off).reshape(b, l, h, p)


def ssd_mixer(z, xbc, dt_raw, conv_w, conv_b, dt_bias, a_log, d_skip, norm_w):
    b, l, _ = z.shape
    xbc = jax.nn.silu(causal_depthwise_conv(xbc, conv_w, conv_b))
    xs, bm, cm = jnp.split(xbc, [SSD_INNER, SSD_INNER + SSD_GROUPS * SSD_STATE], axis=-1)
    xs = xs.reshape(b, l, SSD_HEADS, SSD_HEAD_DIM)
    bm = bm.reshape(b, l, SSD_GROUPS, SSD_STATE)
    cm = cm.reshape(b, l, SSD_GROUPS, SSD_STATE)
    dt = jax.nn.softplus((dt_raw + dt_bias).astype(jnp.float32))
    a = -jnp.exp(a_log.astype(jnp.float32))
    y = ssd_chunked(xs, dt, a, bm, cm) + xs * d_skip[:, None]
    y = y.reshape(b, l, SSD_INNER) * jax.nn.silu(z)
    y = rms_norm(y.reshape(b, l, SSD_GROUPS, -1), norm_w.reshape(SSD_GROUPS, -1))
    return y.reshape(b, l, SSD_INNER).astype(z.dtype)


def linear_recurrence_combine(e1, e2):
    a1, b1 = e1
    a2, b2 = e2
    return a1 * a2, a2 * b1 + b2


def s5_mixer(u, lam_re, lam_im, log_dt, b_re, b_im, c_re, c_im, d_skip, w_glu, b_glu):
    b, l, _ = u.shape
    f32 = jnp.float32
    ug = u.reshape(b, l, S5_GROUPS, S5_GROUP).astype(f32)
    lam = lax.complex(lam_re.astype(f32), lam_im.astype(f32))
    dt = jnp.exp(log_dt.astype(f32))[:, None]
    lam_bar = jnp.exp(lam * dt)
    b_bar = ((lam_bar - 1.0) / lam)[..., None] * lax.complex(b_re.astype(f32), b_im.astype(f32))
    bu = jnp.einsum('blgp,gnp->blgn', ug.astype(jnp.complex64), b_bar)
    a = jnp.broadcast_to(lam_bar, (1, l) + lam_bar.shape)
    _, states = lax.associative_scan(linear_recurrence_combine, (a, bu), axis=1)
    c = lax.complex(c_re.astype(f32), c_im.astype(f32))
    y = jnp.real(jnp.einsum('blgn,gpn->blgp', states, c)) + d_skip.astype(f32) * ug
    y = jax.nn.gelu(y.reshape(b, l, S5_WIDTH))
    y = y * jax.nn.sigmoid(y @ w_glu.astype(f32) + b_glu.astype(f32))
    return y.astype(u.dtype)


def gla_chunked(q, k, v, log_f):
    b, l, h, dk = q.shape
    dv = v.shape[-1]
    nc = l // LIN_CHUNK

    def to_chunks(t):
        return jnp.moveaxis(t.reshape(b, nc, LIN_CHUNK, *t.shape[2:]), 1, 0)

    qc, kc, vc, gc = to_chunks(q), to_chunks(k), to_chunks(v), to_chunks(log_f.astype(jnp.float32))
    causal = jnp.tril(jnp.ones((LIN_CHUNK, LIN_CHUNK), bool))[None, :, :, None, None]

    def step(s, inp):
        q_, k_, v_, g_ = inp
        gcum = jnp.cumsum(g_, axis=1)
        diff = gcum[:, :, None] - gcum[:, None, :]
        decay = jnp.exp(jnp.where(causal, diff, -jnp.inf))
        attn = jnp.einsum('blshd,bshd->bhls', decay * q_[:, :, None], k_)
        o = (jnp.einsum('bhls,bshv->blhv', attn, v_)
             + jnp.einsum('blhd,bhdv->blhv', q_ * jnp.exp(gcum), s))
        g_last = gcum[:, -1]
        s = (jnp.exp(g_last)[..., None] * s
             + jnp.einsum('bshd,bshv->bhdv', k_ * jnp.exp(g_last[:, None] - gcum), v_))
        return s, o

    s0 = jnp.zeros((b, h, dk, dv), jnp.float32)
    _, o = lax.scan(step, s0, (qc, kc, vc, gc))
    return jnp.moveaxis(o, 0, 1).reshape(b, l, h, dv).astype(v.dtype)


def hgrn_lower_bound(lb_logits, layer):
    cum = jnp.cumsum(jax.nn.softmax(lb_logits.astype(jnp.float32), axis=0), axis=0)
    return cum[layer] - cum[0]


def mixer_ab(x, w_in, w_out, conv_w, conv_b, dt_bias, a_log, d_skip, norm_w,
             lam_re, lam_im, log_dt, b_re, b_im, c_re, c_im, s5_d, w_glu, b_glu):
    z, xbc, dt_raw, u = split_cols(x @ w_in, AB_SPLITS)
    y_a = ssd_mixer(z, xbc, dt_raw, conv_w, conv_b, dt_bias, a_log, d_skip, norm_w)
    y_b = s5_mixer(u, lam_re, lam_im, log_dt, b_re, b_im, c_re, c_im, s5_d, w_glu, b_glu)
    return jnp.concatenate([y_a, y_b], axis=-1) @ w_out


def mixer_cd(x, w_in, w_out, lb, hgrn_norm_w, gla_w_gate_up, gla_b_gate, gla_norm_w):
    b, l, _ = x.shape
    hq, hf, hi, hg, gq, gk, gv, glr, gr = split_cols(x @ w_in, CD_SPLITS)
    q = jax.nn.silu(hq).reshape(b, l, HGRN_HEADS, HGRN_KEY)
    f_logit = hf.astype(jnp.float32).reshape(b, l, HGRN_HEADS, HGRN_KEY)
    lb = lb.reshape(HGRN_HEADS, HGRN_KEY)
    log_f = jnp.logaddexp(jnp.log(lb), jnp.log1p(-lb) + jax.nn.log_sigmoid(f_logit))
    k = (1.0 - lb) * jax.nn.sigmoid(-f_logit)
    v = hi.reshape(b, l, HGRN_HEADS, HGRN_VAL)
    o_c = gla_chunked(q, k, v, log_f)
    o_c = rms_norm(o_c, hgrn_norm_w.reshape(HGRN_HEADS, HGRN_VAL)).reshape(b, l, HGRN_WIDTH) * jax.nn.silu(hg)
    q = gq.reshape(b, l, GLA_HEADS, GLA_DK) * (GLA_DK ** -0.5)
    k = gk.reshape(b, l, GLA_HEADS, GLA_DK)
    v = gv.reshape(b, l, GLA_HEADS, GLA_DV)
    log_a = jax.nn.log_sigmoid((glr @ gla_w_gate_up + gla_b_gate).astype(jnp.float32)) / GLA_TAU
    o_d = gla_chunked(q, k, v, log_a.reshape(b, l, GLA_HEADS, GLA_DK))
    o_d = rms_norm(o_d, gla_norm_w.reshape(GLA_HEADS, GLA_DV)).reshape(b, l, GLA_WIDTH) * jax.nn.silu(gr)
    return jnp.concatenate([o_c.astype(x.dtype), o_d.astype(x.dtype)], axis=-1) @ w_out


def setup_inputs(seed: int = 0) -> dict:
    key = jax.random.key(seed)
    ks = iter(jax.random.split(key, 48))
    f32 = jnp.float32

    def nrm(shape, scale):
        return scale * jax.random.normal(next(ks), shape, f32)

    def unif(shape, lo, hi):
        return jax.random.uniform(next(ks), shape, f32, minval=lo, maxval=hi)

    dt0 = jnp.exp(unif((N_EVEN, SSD_HEADS), math.log(1e-3), math.log(1e-1)))
    return {
        "x": nrm((BATCH, SEQ, D_MODEL), 1.0),
        "p": nrm((DEPTH, BATCH, SEQ, PLE_DIM), 1.0),
        "ln_g": 1.0 + nrm((DEPTH, 3, D_MODEL), 0.02),
        "ln_b": nrm((DEPTH, 3, D_MODEL), 0.02),
        "ffn_w_gate": nrm((DEPTH, 2, D_MODEL, D_FF), D_MODEL ** -0.5),
        "ffn_w_up": nrm((DEPTH, 2, D_MODEL, D_FF), D_MODEL ** -0.5),
        "ffn_w_down": nrm((DEPTH, 2, D_FF, D_MODEL), DN_BETA * D_FF ** -0.5),
        "ple_w_gate": nrm((DEPTH, D_MODEL, D_MODEL), D_MODEL ** -0.5),
        "ple_w_proj": nrm((DEPTH, PLE_DIM, D_MODEL), PLE_DIM ** -0.5),
        "ab_w_in": nrm((N_EVEN, D_MODEL, AB_IN), D_MODEL ** -0.5),
        "ab_w_out": nrm((N_EVEN, AB_OUT, D_MODEL), DN_BETA * AB_OUT ** -0.5),
        "ssd_conv_w": nrm((N_EVEN, SSD_CONV, SSD_CONV_DIM), SSD_CONV ** -0.5),
        "ssd_conv_b": nrm((N_EVEN, SSD_CONV_DIM), 0.02),
        "ssd_dt_bias": dt0 + jnp.log(-jnp.expm1(-dt0)),
        "ssd_a_log": jnp.log(unif((N_EVEN, SSD_HEADS), 1.0, 16.0)),
        "ssd_d": 1.0 + nrm((N_EVEN, SSD_HEADS), 0.02),
        "ssd_norm_w": 1.0 + nrm((N_EVEN, SSD_INNER), 0.02),
        "s5_lambda_re": -0.5 + nrm((N_EVEN, S5_GROUPS, S5_STATE), 0.01),
        "s5_lambda_im": jnp.pi * jnp.arange(S5_STATE, dtype=f32) + nrm((N_EVEN, S5_GROUPS, S5_STATE), 0.01),
        "s5_log_dt": unif((N_EVEN, S5_GROUPS), math.log(1e-3), math.log(1e-1)),
        "s5_b_re": nrm((N_EVEN, S5_GROUPS, S5_STATE, S5_GROUP), (2.0 * S5_GROUP) ** -0.5),
        "s5_b_im": nrm((N_EVEN, S5_GROUPS, S5_STATE, S5_GROUP), (2.0 * S5_GROUP) ** -0.5),
        "s5_c_re": nrm((N_EVEN, S5_GROUPS, S5_GROUP, S5_STATE), S5_STATE ** -0.5),
        "s5_c_im": nrm((N_EVEN, S5_GROUPS, S5_GROUP, S5_STATE), S5_STATE ** -0.5),
        "s5_d": nrm((N_EVEN, S5_GROUPS, S5_GROUP), 1.0),
        "s5_w_glu": nrm((N_EVEN, S5_WIDTH, S5_WIDTH), S5_WIDTH ** -0.5),
        "s5_b_glu": nrm((N_EVEN, S5_WIDTH), 0.02),
        "cd_w_in": nrm((N_ODD, D_MODEL, CD_IN), D_MODEL ** -0.5),
        "cd_w_out": nrm((N_ODD, CD_OUT, D_MODEL), DN_BETA * CD_OUT ** -0.5),
        "hgrn_lb_logits": nrm((DEPTH, HGRN_HEADS * HGRN_KEY), 0.5),
        "hgrn_norm_w": 1.0 + nrm((N_ODD, HGRN_WIDTH), 0.02),
        "gla_w_gate_up": nrm((N_ODD, GLA_RANK, GLA_HEADS * GLA_DK), GLA_RANK ** -0.5),
        "gla_b_gate": nrm((N_ODD, GLA_HEADS * GLA_DK), 0.1),
        "gla_norm_w": 1.0 + nrm((N_ODD, GLA_WIDTH), 0.02),
    }


def reference(x, p, ln_g, ln_b, ffn_w_gate, ffn_w_up, ffn_w_down, ple_w_gate, ple_w_proj,
              ab_w_in, ab_w_out, ssd_conv_w, ssd_conv_b, ssd_dt_bias, ssd_a_log, ssd_d, ssd_norm_w,
              s5_lambda_re, s5_lambda_im, s5_log_dt, s5_b_re, s5_b_im, s5_c_re, s5_c_im, s5_d,
              s5_w_glu, s5_b_glu, cd_w_in, cd_w_out, hgrn_lb_logits, hgrn_norm_w,
              gla_w_gate_up, gla_b_gate, gla_norm_w):
    for i in range(DEPTH):
        j = i // 2
        x = layer_norm(DN_ALPHA * x + 0.5 * swiglu(x, ffn_w_gate[i, 0], ffn_w_up[i, 0], ffn_w_down[i, 0]),
                       ln_g[i, 0], ln_b[i, 0])
        if i % 2 == 0:
            mix = mixer_ab(x, ab_w_in[j], ab_w_out[j], ssd_conv_w[j], ssd_conv_b[j], ssd_dt_bias[j],
                           ssd_a_log[j], ssd_d[j], ssd_norm_w[j], s5_lambda_re[j], s5_lambda_im[j],
                           s5_log_dt[j], s5_b_re[j], s5_b_im[j], s5_c_re[j], s5_c_im[j], s5_d[j],
                           s5_w_glu[j], s5_b_glu[j])
        else:
            mix = mixer_cd(x, cd_w_in[j], cd_w_out[j], hgrn_lower_bound(hgrn_lb_logits, i),
                           hgrn_norm_w[j], gla_w_gate_up[j], gla_b_gate[j], gla_norm_w[j])
        x = layer_norm(DN_ALPHA * x + mix, ln_g[i, 1], ln_b[i, 1])
        x = layer_norm(DN_ALPHA * x + 0.5 * swiglu(x, ffn_w_gate[i, 1], ffn_w_up[i, 1], ffn_w_down[i, 1]),
                       ln_g[i, 2], ln_b[i, 2])
        x = x + jax.nn.sigmoid(x @ ple_w_gate[i]) * (p[i] @ ple_w_proj[i])
    return x
```

```python
import contextlib
import numpy as np
import concourse.bass as bass
import concourse.mybir as mybir
from concourse.bass_utils import run_bass_kernel_spmd

F32 = mybir.dt.float32
BF16 = mybir.dt.bfloat16
I32 = mybir.dt.int32
AF = mybir.ActivationFunctionType
ALU = mybir.AluOpType

D = 1024
T = 2048
DFF = 2816
NF = DFF // 128
DEPTH = 2
ALPHA = (2.0 * DEPTH) ** 0.25
LN_EPS = 1e-5
EPS_P = LN_EPS / (ALPHA * ALPHA)

DEBUG = False
N_DMA_SEMS = 48
SAME_ENGINE_SYNC = True
ENGS = ("pe", "dve", "act", "pool", "sp")


class Buf:
    __slots__ = ("name", "lw", "rd", "dma_rd")

    def __init__(self, name=""):
        self.name = name
        self.lw = None
        self.rd = {}
        self.dma_rd = []


class Op:
    __slots__ = ("eng", "fn", "deps", "is_dma", "signal", "sigval", "dslot", "dval", "idx", "gid")

    def __init__(self, eng, fn, is_dma):
        self.eng = eng
        self.fn = fn
        self.deps = []
        self.is_dma = is_dma
        self.signal = False
        self.sigval = 0
        self.dslot = -1
        self.dval = 0
        self.idx = -1
        self.gid = -1


class Prog:
    def __init__(self, nc):
        self.nc = nc
        self.ops = {e: [] for e in ENGS}
        self.seen = {e: {x: -1 for x in ENGS} for e in ENGS}
        self.dma_slot_last = [None] * N_DMA_SEMS
        self.dma_slot_cnt = [0] * N_DMA_SEMS
        self.dma_rr = 0
        self.nops = 0
        self.dma_seen = {e: set() for e in ENGS}
        self.pending = {e: [] for e in ENGS}

    def barrier(self):
        last = [self.ops[e][-1] for e in ENGS if self.ops[e]]
        last += [o for o in self.dma_slot_last if o is not None]
        for e in ENGS:
            self.pending[e] = list(last)

    def _add_dep(self, op, p):
        if p is None or p is op:
            return
        E = op.eng
        if p.is_dma:
            if p.gid in self.dma_seen[E]:
                return
            self.dma_seen[E].add(p.gid)
            op.deps.append(p)
            return
        if p.eng == E and (E == "pe" or E == "sp" or not SAME_ENGINE_SYNC):
            return
        if self.seen[E][p.eng] >= p.idx:
            return
        self.seen[E][p.eng] = p.idx
        op.deps.append(p)

    def op(self, eng, fn, reads=(), writes=(), dma=False):
        o = Op(eng, fn, dma)
        o.idx = len(self.ops[eng])
        o.gid = self.nops
        self.nops += 1
        if self.pending[eng]:
            for p in self.pending[eng]:
                self._add_dep(o, p)
            self.pending[eng] = []
        for b in reads:
            self._add_dep(o, b.lw)
        for b in writes:
            self._add_dep(o, b.lw)
            for r in b.rd.values():
                if r.eng != eng:
                    self._add_dep(o, r)
            for r in b.dma_rd:
                self._add_dep(o, r)
        if dma:
            k = self.dma_rr
            self.dma_rr = (self.dma_rr + 1) % N_DMA_SEMS
            self._add_dep(o, self.dma_slot_last[k])
            self.dma_slot_cnt[k] += 16
            o.dslot = k
            o.dval = self.dma_slot_cnt[k]
            self.dma_slot_last[k] = o
        for b in reads:
            if dma:
                b.dma_rd.append(o)
            else:
                b.rd[eng] = o
        for b in writes:
            b.lw = o
            b.rd = {}
            b.dma_rd = []
        self.ops[eng].append(o)
        return o

    def dma(self, out, in_, reads=(), writes=(), eng="sp", **kw):
        return self.op(eng, lambda e: e.dma_start(out=out, in_=in_, **kw), reads, writes, dma=True)

    def mm(self, out, lhsT, rhs, start, stop, reads=(), writes=(), **kw):
        return self.op("pe", lambda e: e.matmul(out, lhsT, rhs, start=start, stop=stop, **kw), reads, writes)

    def transpose(self, out, in_, ident, reads=(), writes=()):
        return self.op("pe", lambda e: e.transpose(out, in_, ident), reads, writes)

    def scan(self, out, data0, data1, initial, reads=(), writes=()):
        return self.op("dve", lambda e: e.tensor_tensor_scan(out=out, data0=data0, data1=data1, initial=initial,
                                                              op0=ALU.mult, op1=ALU.add), reads, writes)

    def recip(self, out, in_, reads=(), writes=()):
        return self.op("dve", lambda e: e.reciprocal(out=out, in_=in_), reads, writes)

    def act(self, out, in_, func, reads=(), writes=(), **kw):
        return self.op("act", lambda e: e.activation(out=out, in_=in_, func=func, **kw), reads, writes)

    def tt(self, eng, out, in0, in1, op, reads=(), writes=()):
        return self.op(eng, lambda e: e.tensor_tensor(out=out, in0=in0, in1=in1, op=op), reads, writes)

    def ts(self, eng, out, in0, s1, s2, op0, op1=None, reads=(), writes=()):
        if op1 is None:
            return self.op(eng, lambda e: e.tensor_scalar(out=out, in0=in0, scalar1=s1, scalar2=None, op0=op0), reads, writes)
        return self.op(eng, lambda e: e.tensor_scalar(out=out, in0=in0, scalar1=s1, scalar2=s2, op0=op0, op1=op1), reads, writes)

    def stt(self, out, in0, scalar, in1, op0, op1, reads=(), writes=()):
        return self.op("dve", lambda e: e.scalar_tensor_tensor(out=out, in0=in0, scalar=scalar, in1=in1, op0=op0, op1=op1), reads, writes)

    def copy(self, eng, out, in_, reads=(), writes=()):
        if eng == "act":
            return self.op(eng, lambda e: e.copy(out=out, in_=in_), reads, writes)
        return self.op(eng, lambda e: e.tensor_copy(out=out, in_=in_), reads, writes)

    def memset(self, eng, ap, val, writes=()):
        return self.op(eng, lambda e: e.memset(ap, val), (), writes)

    def emit(self):
        nc = self.nc
        for e in ENGS:
            for o in self.ops[e]:
                for p in o.deps:
                    if not p.is_dma:
                        p.signal = True
        for e in ENGS:
            c = 0
            for o in self.ops[e]:
                if o.signal and not o.is_dma:
                    c += 1
                    o.sigval = c
        with contextlib.ExitStack() as st:
            esem = {e: st.enter_context(nc.semaphore("s_" + e)) for e in ENGS}
            dsem = [st.enter_context(nc.semaphore("d%d" % k)) for k in range(N_DMA_SEMS)]
            block = st.enter_context(nc.Block())
            engobj = {"pe": "tensor", "dve": "vector", "act": "scalar", "pool": "gpsimd", "sp": "sync"}

            def make(ename):
                def body(eng):
                    for o in self.ops[ename]:
                        for p in o.deps:
                            if p.is_dma:
                                eng.wait_ge(dsem[p.dslot], p.dval)
                            else:
                                eng.wait_ge(esem[p.eng], p.sigval)
                        ins = o.fn(eng)
                        if o.is_dma:
                            ins.then_inc(dsem[o.dslot], 16)
                        elif o.signal:
                            ins.then_inc(esem[ename], 1)
                    if ename == "sp":
                        for k in range(N_DMA_SEMS):
                            if self.dma_slot_cnt[k]:
                                eng.wait_ge(dsem[k], self.dma_slot_cnt[k])
                return body

            for ename in ENGS:
                getattr(block, engobj[ename])(make(ename))


def V(ap, a):
    return ap.rearrange("p (a b) -> p a b", a=a)


class K:
    def __init__(self, stages):
        self.stages = stages
        nc = self.nc = bass.Bass("TRN2", target_bir_lowering=False)
        self.P = Prog(nc)
        self.st = contextlib.ExitStack()
        di = lambda name, shape: nc.dram_tensor(name, list(shape), F32, kind="ExternalInput").ap()
        do = lambda name, shape: nc.dram_tensor(name, list(shape), F32, kind="ExternalOutput").ap()

        def dip(name, shape):
            n = int(np.prod(shape))
            flat = nc.dram_tensor(name, [n + 16], F32, kind="ExternalInput").ap()
            letters = "abcdefg"[:len(shape)]
            pat = "(%s) -> %s" % (" ".join(letters), " ".join(letters))
            return flat[0:n].rearrange(pat, **{l: int(s) for l, s in zip(letters[1:], shape[1:])})

        self.xT = di("xT", [D, T])
        self.pT = di("pT", [DEPTH, 256, T])
        self.lngb = di("lngb", [128, DEPTH * 3 * 2 * 8])
        self.w_gate = dip("ffn_w_gate", [DEPTH, 2, D, DFF])
        self.w_up = dip("ffn_w_up", [DEPTH, 2, D, DFF])
        self.w_down = dip("ffn_w_down", [DEPTH, 2, DFF, D])
        self.ple_wg = dip("ple_w_gate", [DEPTH, D, D])
        self.ple_wp = dip("ple_w_proj", [DEPTH, 256, D])
        self.consts = di("consts", [128, 256])
        self.ab_w_in = dip("ab_w_in", [D, 4112])
        self.ab_w_out = dip("ab_w_out", [2048, D])
        self.s5_w_glu = dip("s5_w_glu", [D, D])
        self.ab_small = di("ab_small", [128, 112])
        self.ssd16 = di("ssd16", [16, 2])
        self.s5lam = di("s5lam", [128, 3, 32])
        self.s5B = dip("s5B", [128, 32, 2, 128])
        self.s5C = dip("s5C", [128, 32, 2, 32])
        self.cd_w_in = dip("cd_w_in", [D, 3600])
        self.cd_w_out = dip("cd_w_out", [D, D])
        self.cd_small = di("cd_small", [128, 18])
        self.gla_wup = di("gla_wup", [16, 256])
        self.init_gla = di("init_gla", [128, 768])
        self.fin_gla = do("fin_gla", [128, 768])
        self.init_ssd = di("init_ssd", [128, 1024])
        self.init_conv = di("init_conv", [128, 16, 3])
        self.init_s5 = di("init_s5", [128, 64])
        self.fin_ssd = do("fin_ssd", [128, 1024])
        self.fin_conv = do("fin_conv", [128, 16, 3])
        self.fin_s5 = do("fin_s5", [128, 64])
        self.out = do("outT", [D, T])
        self.dbg = {n: do("dbg_" + n, [128, 8, T]) for n in ("y", "xs", "z", "cat", "bc")} if DEBUG else {}
        if DEBUG:
            self.dbg["tok"] = do("dbg_tok", [128, 4, 4 * 48])
            self.dbg["cd"] = do("dbg_cd", [128, 4, 64])
            self.dbg["acumT"] = do("dbg_acumT", [16, T])
            self.dbg["dtT"] = do("dbg_dtT", [16, T])
            self.dbg["smask"] = do("dbg_smask", [128, 512])
            self.dbg["xstok"] = do("dbg_xstok", [128, 1024])
            self.dbg["btok"] = do("dbg_btok", [128, 512])
            self.dbg["xdte"] = do("dbg_xdte", [128, 1024])
            for n in ("E", "MT", "Er", "Ch"):
                self.dbg[n] = do("dbg_" + n, [128, 16, 128])
        self.xpark = nc.dram_tensor("xpark", [D, T], F32, kind="Internal").ap()
        self.Bxpark = [[Buf() for _ in range(4)] for _ in range(8)]
        self.lnp = self.sb("lnp", [128, DEPTH * 3 * 2 * 8], F32)
        self.Blnp = Buf("lnp")
        self.ones32 = self.sb("ones32", [128, 128], F32)
        self.Bones32 = Buf("ones32")
        self.cst = self.sb("cst", [128, 256], F32)
        self.Bcst = Buf("cst")
        self.ident32 = self.cst[:, 0:128]
        self.maskT = self.cst[:, 128:256]
        self.identb = self.sb("identb", [128, 128], BF16)
        self.Bidentb = Buf("identb")
        self.ps = [self.st.enter_context(nc.psum_tensor("ps%d" % i, [128, 512], F32)) for i in range(8)]
        self.Bps = [Buf("ps%d" % i) for i in range(8)]
        self.x32 = None

    def sb(self, name, shape, dt, stack=None):
        self.uid = getattr(self, "uid", 0) + 1
        return (stack or self.st).enter_context(self.nc.sbuf_tensor("%s_%d" % (name, self.uid), list(shape), dt))

    def prologue(self):
        P = self.P
        P.dma(self.lnp[:], self.lngb, writes=[self.Blnp])
        P.dma(self.cst[:], self.consts, writes=[self.Bcst])
        P.memset("dve", self.ones32[:], 1.0, writes=[self.Bones32])
        P.copy("dve", self.identb[:], self.ident32, reads=[self.Bcst], writes=[self.Bidentb])

    def open_x32(self, src):
        P = self.P
        self.xstack = contextlib.ExitStack()
        self.x32 = self.sb("x32", [128, 8, T], F32, self.xstack)
        self.Bx32 = [[Buf("x32_%d_%d" % (k, t)) for t in range(4)] for k in range(8)]
        for t in range(4):
            for k in range(8):
                P.dma(self.x32[:, k, t * 512:(t + 1) * 512], src[k * 128:(k + 1) * 128, t * 512:(t + 1) * 512],
                      reads=([self.Bxpark[k][t]] if src is self.xpark else []), writes=[self.Bx32[k][t]])

    def close_x32(self, dst):
        P = self.P
        for t in range(4):
            for k in range(8):
                P.dma(dst[k * 128:(k + 1) * 128, t * 512:(t + 1) * 512], self.x32[:, k, t * 512:(t + 1) * 512],
                      reads=[self.Bx32[k][t]], writes=([self.Bxpark[k][t]] if dst is self.xpark else []))
        P.barrier()
        self.xstack.close()
        self.x32 = None

    def ln_g(self, l, i, k):
        c = ((l * 3 + i) * 2 + 0) * 8 + k
        return self.lnp[:, c:c + 1]

    def ln_b(self, l, i, k):
        c = ((l * 3 + i) * 2 + 1) * 8 + k
        return self.lnp[:, c:c + 1]

    def layer_norm_tile(self, l, i, xap, Bx, tmp, Btmp):
        P = self.P
        ps1, ps2 = self.ps[6], self.ps[7]
        B1, B2 = self.Bps[6], self.Bps[7]
        for k in range(8):
            P.mm(ps1[:], self.ones32[:], xap(k), k == 0, k == 7,
                 reads=[self.Bones32, Bx(k)], writes=[B1])
        for k in range(8):
            sq, Bsq = tmp["sq%d" % (k % 2)], Btmp["sq%d" % (k % 2)]
            P.act(sq[:], xap(k), AF.Square, reads=[Bx(k)], writes=[Bsq])
            P.mm(ps2[:], self.ones32[:], sq[:], k == 0, k == 7, reads=[self.Bones32, Bsq], writes=[B2])
        mean, Bm = tmp["mean"], Btmp["mean"]
        rstd, Br = tmp["rstd"], Btmp["rstd"]
        P.ts("dve", mean[:], ps1[:], 1.0 / D, None, ALU.mult, reads=[B1], writes=[Bm])
        P.tt("dve", rstd[:], mean[:], mean[:], ALU.mult, reads=[Bm], writes=[Br])
        P.stt(rstd[:], ps2[:], 1.0 / D, rstd[:], ALU.mult, ALU.subtract, reads=[B2, Br], writes=[Br])
        P.ts("dve", rstd[:], rstd[:], EPS_P, None, ALU.add, reads=[Br], writes=[Br])
        P.act(rstd[:], rstd[:], AF.Sqrt, reads=[Br], writes=[Br])
        P.recip(rstd[:], rstd[:], reads=[Br], writes=[Br])
        for k in range(8):
            eng = "dve" if k % 2 == 0 else "pool"
            xc, Bxc = tmp["xc%d" % (k % 2)], Btmp["xc%d" % (k % 2)]
            P.tt(eng, xc[:], xap(k), mean[:], ALU.subtract, reads=[Bx(k), Bm], writes=[Bxc])
            P.tt(eng, xc[:], xc[:], rstd[:], ALU.mult, reads=[Bxc, Br], writes=[Bxc])
            P.act(xap(k), xc[:], AF.Identity, scale=self.ln_g(l, i, k), bias=self.ln_b(l, i, k),
                  reads=[Bxc, self.Blnp], writes=[Bx(k)])

    def ln_tmp(self, ls, pre):
        tmp, Btmp = {}, {}
        for n in ("sq0", "sq1", "mean", "rstd", "xc0", "xc1"):
            tmp[n] = self.sb(pre + n, [128, 512], F32, ls)
            Btmp[n] = Buf()
        return tmp, Btmp

    def ffn(self, l, j):
        P, nc = self.P, self.nc
        ln_i = 0 if j == 0 else 2
        wg = self.w_gate[l, j]
        wu = self.w_up[l, j]
        wd = self.w_down[l, j]
        with contextlib.ExitStack() as ls:
            xb = self.sb("f_xb", [128, 8, 1024], BF16, ls)
            Bxb = [[Buf() for _ in range(2)] for _ in range(8)]
            aT = self.sb("f_aT", [128, NF, 1024], BF16, ls)
            BaT = [[Buf() for _ in range(2)] for _ in range(NF)]
            wgs = [self.sb("f_wg%d" % i, [128, 8, 256], BF16, ls) for i in range(2)]
            wus = [self.sb("f_wu%d" % i, [128, 8, 256], BF16, ls) for i in range(2)]
            Bwgs = [Buf() for _ in range(2)]
            Bwus = [Buf() for _ in range(2)]
            wds = [self.sb("f_wd%d" % i, [128, NF, 512], BF16, ls) for i in range(2)]
            Bwds = [Buf() for _ in range(2)]
            sg = [self.sb("f_sg%d" % i, [128, 512], F32, ls) for i in range(2)]
            Bsg = [Buf() for _ in range(2)]
            tmp, Btmp = self.ln_tmp(ls, "f_")
            pcnt = 0
            for s in range(2):
                for k in range(8):
                    for tt in range(2):
                        t = s * 2 + tt
                        eng = ("dve", "act")[(k * 2 + tt) % 2]
                        P.copy(eng, xb[:, k, tt * 512:(tt + 1) * 512], self.x32[:, k, t * 512:(t + 1) * 512],
                               reads=[self.Bx32[k][t]], writes=[Bxb[k][tt]])
                for f2 in range(NF // 2):
                    bi = f2 % 2
                    c0 = f2 * 256
                    P.dma(wgs[bi][:], wg[:, c0:c0 + 256].rearrange("(k p) c -> p k c", p=128), writes=[Bwgs[bi]], eng="pool")
                    P.dma(wus[bi][:], wu[:, c0:c0 + 256].rearrange("(k p) c -> p k c", p=128), writes=[Bwus[bi]], eng="pool")
                    for fi in range(2):
                        f = f2 * 2 + fi
                        for tt in range(2):
                            pg, Bpg = self.ps[(pcnt % 2) * 2], self.Bps[(pcnt % 2) * 2]
                            pu, Bpu = self.ps[(pcnt % 2) * 2 + 1], self.Bps[(pcnt % 2) * 2 + 1]
                            sgi, Bsgi = sg[pcnt % 2], Bsg[pcnt % 2]
                            pcnt += 1
                            for k in range(8):
                                P.mm(pg[:], wgs[bi][:, k, fi * 128:(fi + 1) * 128], xb[:, k, tt * 512:(tt + 1) * 512],
                                     k == 0, k == 7, reads=[Bwgs[bi], Bxb[k][tt]], writes=[Bpg])
                            for k in range(8):
                                P.mm(pu[:], wus[bi][:, k, fi * 128:(fi + 1) * 128], xb[:, k, tt * 512:(tt + 1) * 512],
                                     k == 0, k == 7, reads=[Bwus[bi], Bxb[k][tt]], writes=[Bpu])
                            P.act(sgi[:], pg[:], AF.Silu, reads=[Bpg], writes=[Bsgi])
                            P.tt("dve", aT[:, f, tt * 512:(tt + 1) * 512], sgi[:], pu[:], ALU.mult,
                                 reads=[Bsgi, Bpu], writes=[BaT[f][tt]])
                for h in range(2):
                    for q in range(2):
                        P.dma(wds[h][:, q * 11:(q + 1) * 11, :],
                              wd[q * 11 * 128:(q + 1) * 11 * 128, h * 512:(h + 1) * 512].rearrange("(f p) c -> p f c", p=128),
                              writes=[Bwds[h]], eng="pool")
                    for dc in range(4):
                        kk = h * 4 + dc
                        for tt in range(2):
                            t = s * 2 + tt
                            py, Bpy = self.ps[4 + (pcnt % 2)], self.Bps[4 + (pcnt % 2)]
                            pcnt += 1
                            for f in range(NF):
                                P.mm(py[:], wds[h][:, f, dc * 128:(dc + 1) * 128], aT[:, f, tt * 512:(tt + 1) * 512],
                                     f == 0, f == NF - 1, reads=[Bwds[h], BaT[f][tt]], writes=[Bpy])
                            xs = self.x32[:, kk, t * 512:(t + 1) * 512]
                            P.stt(xs, py[:], 0.5 / ALPHA, xs, ALU.mult, ALU.add,
                                  reads=[Bpy, self.Bx32[kk][t]], writes=[self.Bx32[kk][t]])
                for tt in range(2):
                    t = s * 2 + tt
                    self.layer_norm_tile(l, ln_i, lambda k, t=t: self.x32[:, k, t * 512:(t + 1) * 512],
                                         lambda k, t=t: self.Bx32[k][t], tmp, Btmp)
            P.barrier()

    def ple(self, l):
        P = self.P
        with contextlib.ExitStack() as ls:
            xb = self.sb("p_xb", [128, 8, 512], BF16, ls)
            Bxb = [Buf() for _ in range(8)]
            pb = self.sb("p_pb", [128, 2, T], BF16, ls)
            Bpb = Buf()
            wg = self.sb("p_wg", [128, 8, D], BF16, ls)
            Bwg = Buf()
            wp = self.sb("p_wp", [128, 2, D], BF16, ls)
            Bwp = Buf()
            sg = [self.sb("p_sg%d" % i, [128, 512], F32, ls) for i in range(2)]
            Bsg = [Buf() for _ in range(2)]
            for q in range(4):
                P.dma(wg[:, q * 2:(q + 1) * 2, :], self.ple_wg[l, q * 256:(q + 1) * 256, :].rearrange("(k p) c -> p k c", p=128),
                      writes=[Bwg], eng="pool")
            P.dma(wp[:], self.ple_wp[l].rearrange("(k p) c -> p k c", p=128), writes=[Bwp], eng="pool")
            P.dma(pb[:], self.pT[l].rearrange("(k p) t -> p k t", p=128), writes=[Bpb], eng="pool")
            pcnt = 0
            for t in range(4):
                sl = slice(t * 512, (t + 1) * 512)
                for k in range(8):
                    eng = ("dve", "act")[k % 2]
                    P.copy(eng, xb[:, k, :], self.x32[:, k, sl], reads=[self.Bx32[k][t]], writes=[Bxb[k]])
                for dc in range(8):
                    pg, Bpg = self.ps[(pcnt % 2) * 2], self.Bps[(pcnt % 2) * 2]
                    pp, Bpp = self.ps[(pcnt % 2) * 2 + 1], self.Bps[(pcnt % 2) * 2 + 1]
                    sgi, Bsgi = sg[pcnt % 2], Bsg[pcnt % 2]
                    pcnt += 1
                    for k in range(8):
                        P.mm(pg[:], wg[:, k, dc * 128:(dc + 1) * 128], xb[:, k, :], k == 0, k == 7,
                             reads=[Bwg, Bxb[k]], writes=[Bpg])
                    for k in range(2):
                        P.mm(pp[:], wp[:, k, dc * 128:(dc + 1) * 128], pb[:, k, sl], k == 0, k == 1,
                             reads=[Bwp, Bpb], writes=[Bpp])
                    P.act(sgi[:], pg[:], AF.Sigmoid, reads=[Bpg], writes=[Bsgi])
                    P.tt("dve", sgi[:], sgi[:], pp[:], ALU.mult, reads=[Bsgi, Bpp], writes=[Bsgi])
                    xs = self.x32[:, dc, sl]
                    P.tt("pool", xs, xs, sgi[:], ALU.add, reads=[self.Bx32[dc][t], Bsgi], writes=[self.Bx32[dc][t]])
            P.barrier()

    def outproj_tile(self, l, w_out, nck, ycat_ap, Bycat, xt32, Bxt, wos, Bwos, tmp, Btmp, t, cnt):
        P = self.P
        for hh in range(2):
            wo, Bwo = wos[cnt[0] % 2], Bwos[cnt[0] % 2]
            cnt[0] += 1
            nq = nck // 8
            for q in range(nq):
                P.dma(wo[:, q * 8:(q + 1) * 8, :],
                      w_out[q * 1024:(q + 1) * 1024, hh * 512:(hh + 1) * 512].rearrange("(c p) n -> p c n", p=128),
                      writes=[Bwo], eng="pool")
            for dc in range(4):
                kk = hh * 4 + dc
                po, Bpo = self.ps[4 + (kk % 2)], self.Bps[4 + (kk % 2)]
                for c in range(nck):
                    P.mm(po[:], wo[:, c, dc * 128:(dc + 1) * 128], ycat_ap(c), c == 0, c == nck - 1,
                         reads=[Bwo, Bycat(c)], writes=[Bpo])
                P.stt(xt32[:, kk, :], po[:], 1.0 / ALPHA, xt32[:, kk, :], ALU.mult, ALU.add,
                      reads=[Bpo, Bxt[kk]], writes=[Bxt[kk]])
        self.layer_norm_tile(l, 1, lambda k: xt32[:, k, :], lambda k: Bxt[k], tmp, Btmp)
        for k in range(8):
            P.dma(self.xpark[k * 128:(k + 1) * 128, t * 512:(t + 1) * 512], xt32[:, k, :], reads=[Bxt[k]],
                  writes=[self.Bxpark[k][t]])

    def mixer_ab(self, do_s5=True):
        P, nc = self.P, self.nc
        ps, Bps = self.ps, self.Bps
        w_in = self.ab_w_in
        with contextlib.ExitStack() as ms:
            sb = lambda n, shp, dt, stack=None: self.sb("m_" + n, shp, dt, stack or ms)
            small = sb("small", [128, 112], F32)
            Bsmall = Buf()
            P.dma(small[:], self.ab_small, writes=[Bsmall])
            convw = lambda j, tap: small[:, j * 4 + tap:j * 4 + tap + 1]
            convb = lambda j: small[:, 64 + j:65 + j]
            Dexp = lambda k: small[:, 80 + k:81 + k]
            normw = lambda k: small[:, 88 + k:89 + k]
            s5d = lambda k: small[:, 96 + k:97 + k]
            bglu = lambda k: small[:, 104 + k:105 + k]
            ycatB = sb("ycatB", [128, 8, T], BF16)
            BycatB = [[Buf() for _ in range(4)] for _ in range(8)]
            slabs = [sb("slab%d" % i, [128, 8, 512], BF16) for i in range(2)]
            Bslabs = [Buf() for _ in range(2)]
            slab_cnt = [0]

            def load_slab(src, c0, ncols):
                i = slab_cnt[0] % 2
                slab_cnt[0] += 1
                P.dma(slabs[i][:, :, 0:ncols], src[:, c0:c0 + ncols].rearrange("(k p) c -> p k c", p=128),
                      writes=[Bslabs[i]], eng="pool")
                return slabs[i], Bslabs[i]

            xb = sb("xb", [128, 8, 512], BF16)
            Bxb = [Buf() for _ in range(8)]

            if do_s5:
                self.s5_phase(ms, sb, small, Bsmall, s5d, bglu, ycatB, BycatB, load_slab, xb, Bxb)
            else:
                for k in range(8):
                    for t in range(4):
                        P.memset("pool", ycatB[:, k, t * 512:(t + 1) * 512], 0.0, writes=[BycatB[k][t]])
            P.barrier()

            with contextlib.ExitStack() as p2:
                sb2 = lambda n, shp, dt: sb(n, shp, dt, p2)
                p16 = sb2("p16", [16, 4], F32)
                Bp16 = Buf()
                P.dma(p16[:, 0:2], self.ssd16, writes=[Bp16])
                P.act(p16[:, 2:3], p16[:, 1:2], AF.Exp, reads=[Bp16], writes=[Bp16])
                P.ts("dve", p16[:, 2:3], p16[:, 2:3], -1.0, None, ALU.mult, reads=[Bp16], writes=[Bp16])
                sel = sb2("sel", [16, 16, 128], F32)
                Bsel = Buf()
                P.copy("dve", sel[:], self.ident32[0:16, 0:16].unsqueeze(2).broadcast_to([16, 16, 128]),
                       reads=[self.Bcst], writes=[Bsel])
                ones16 = sb2("ones16", [16, 128], F32)
                Bones16 = Buf()
                P.memset("dve", ones16[:], 1.0, writes=[Bones16])
                rmask = sb2("rmask", [16, 512], F32)
                Brmask = Buf()
                P.memset("dve", rmask[:], 1.0, writes=[Brmask])
                P.memset("dve", V(rmask[:], 4)[:, :, 0:1], 0.0, writes=[Brmask])
                wdt = sb2("wdt", [128, 8, 16], BF16)
                Bwdt = Buf()
                P.dma(wdt[:], w_in[:, 3072:3088].rearrange("(k p) c -> p k c", p=128), writes=[Bwdt], eng="pool")
                diagD = sb2("diagD", [128, 8, 128], BF16)
                BdiagD = Buf()
                for k in range(8):
                    P.ts("dve", diagD[:, k, :], self.ident32, Dexp(k), None, ALU.mult, reads=[self.Bcst, Bsmall], writes=[BdiagD])
                S = sb2("S", [128, 1024], F32)
                BS = Buf()
                Sbf = sb2("Sbf", [128, 1024], BF16)
                BSbf = Buf()
                P.dma(S[:], self.init_ssd, writes=[BS])
                P.copy("act", Sbf[:], S[:], reads=[BS], writes=[BSbf])
                halo = sb2("halo", [128, 16, 3], F32)
                Bhalo = [Buf() for _ in range(16)]
                P.dma(halo[:], self.init_conv, writes=Bhalo)
                ycatA = sb2("ycatA", [128, 8, 512], BF16)
                BycatA = [Buf() for _ in range(8)]
                xt32 = sb2("xt32", [128, 8, 512], F32)
                Bxt = [Buf() for _ in range(8)]
                wos = [sb2("wo%d" % i, [128, 16, 512], BF16) for i in range(1)]
                wos = [wos[0], wos[0]]
                Bwo0 = Buf()
                Bwos = [Bwo0, Bwo0]
                wocnt = [0]
                zs = sb2("zs", [128, 8, 512], F32)
                Bzs = [Buf() for _ in range(8)]
                xsT = sb2("xsT", [128, 8, 512], BF16)
                BxsT = [Buf() for _ in range(8)]
                BCT = sb2("BCT", [128, 8, 512], BF16)
                BBCT = [Buf() for _ in range(8)]
                xs_tok = sb2("xs_tok", [128, 1024], BF16)
                Bxs_tok = Buf()
                B_tok = sb2("B_tok", [128, 512], BF16)
                BB_tok = Buf()
                xdte = sb2("xdte", [128, 1024], BF16)
                Bxdte = Buf()
                yT = sb2("yT", [128, 8, 512], F32)
                ByT = [Buf() for _ in range(8)]
                xpre = [sb2("xpre%d" % i, [128, 515], F32) for i in range(2)]
                Bxpre = [Buf() for _ in range(2)]
                cacc = [sb2("cacc%d" % i, [128, 512], F32) for i in range(2)]
                Bcacc = [Buf() for _ in range(2)]
                smask = sb2("smask", [128, 4, 128], F32)
                Bsmask = Buf()
                dif = [sb2("dif%d" % i, [128, 128], F32) for i in range(2)]
                Bdif = [Buf() for _ in range(2)]
                Er = [sb2("Er%d" % i, [128, 128], F32) for i in range(2)]
                BEr = [Buf() for _ in range(2)]
                MT = [sb2("MT%d" % i, [128, 128], BF16) for i in range(2)]
                BMT = [Buf() for _ in range(2)]
                Ch = [sb2("Ch%d" % i, [128, 128], BF16) for i in range(2)]
                BCh = [Buf() for _ in range(2)]
                dtT = sb2("dtT", [16, 512], F32)
                acumT = sb2("acumT", [16, 512], F32)
                sT = sb2("sT", [16, 512], F32)
                d16 = sb2("d16", [16, 512], F32)
                BdtT, BacumT, BsT, Bd16 = Buf(), Buf(), Buf(), Buf()
                dg16 = sb2("dg16", [16, 4, 16], F32)
                Bdg16 = Buf()
                cd = sb2("cd", [128, 4, 16], F32)
                Bcd = Buf()
                tok = sb2("tok", [128, 4, 48], F32)
                Btok = [Buf() for _ in range(4)]
                Barow = [Buf() for _ in range(4)]
                tmp, Btmp = self.ln_tmp(p2, "m_")
                psT = ps[3][:].bitcast(BF16)
                psT2 = ps[2][:].bitcast(BF16)
                pcnt = [0]

                def inproj(slab, Bslab, coff, M=128):
                    i = pcnt[0] % 2
                    pcnt[0] += 1
                    for k in range(8):
                        P.mm(ps[i][0:M, :], slab[:, k, coff:coff + M], xb[:, k, :], k == 0, k == 7,
                             reads=[Bslab, Bxb[k]], writes=[Bps[i]])
                    return ps[i], Bps[i]

                for t in range(4):
                    tsl = slice(t * 512, (t + 1) * 512)
                    for k in range(8):
                        P.dma(xt32[:, k, :], self.xpark[k * 128:(k + 1) * 128, tsl], reads=[self.Bxpark[k][t]], writes=[Bxt[k]])
                    for k in range(8):
                        P.copy(("dve", "act")[k % 2], xb[:, k, :], xt32[:, k, :], reads=[Bxt[k]], writes=[Bxb[k]])
                    pdt, Bpdt = ps[2], Bps[2]
                    for k in range(8):
                        P.mm(pdt[0:16, :], wdt[:, k, :], xb[:, k, :], k == 0, k == 7, reads=[Bwdt, Bxb[k]], writes=[Bpdt])
                    P.act(d16[:], pdt[0:16, :], AF.Exp, bias=p16[:, 0:1], reads=[Bpdt, Bp16], writes=[Bd16])
                    P.act(dtT[:], d16[:], AF.Ln, bias=1.0, reads=[Bd16], writes=[BdtT])
                    P.ts("dve", d16[:], dtT[:], p16[:, 2:3], None, ALU.mult, reads=[BdtT, Bp16], writes=[Bd16])
                    P.scan(acumT[:], rmask[:], d16[:], 0.0, reads=[Brmask, Bd16], writes=[BacumT])
                    tot_b = V(acumT[:], 4)[:, :, 127:128].broadcast_to([16, 4, 128])
                    P.tt("dve", V(d16[:], 4), tot_b, V(acumT[:], 4), ALU.subtract, reads=[BacumT], writes=[Bd16])
                    P.act(d16[:], d16[:], AF.Exp, reads=[Bd16], writes=[Bd16])
                    P.tt("dve", sT[:], dtT[:], d16[:], ALU.mult, reads=[BdtT, Bd16], writes=[BsT])
                    P.tt("dve", dg16[:], V(acumT[:], 4)[:, :, 127:128].broadcast_to([16, 4, 16]),
                         self.ident32[0:16, 0:16].unsqueeze(1).broadcast_to([16, 4, 16]), ALU.mult,
                         reads=[BacumT, self.Bcst], writes=[Bdg16])
                    P.mm(ps[2][:, 0:64], ones16[:], dg16[:].rearrange("p a b -> p (a b)"), True, True,
                         reads=[Bones16, Bdg16], writes=[Bps[2]])
                    P.act(cd[:].rearrange("p a b -> p (a b)"), ps[2][:, 0:64], AF.Exp, reads=[Bps[2]], writes=[Bcd])
                    for half in range(2):
                        slab, Bslab = load_slab(w_in, half * 512, 512)
                        for kq in range(4):
                            k = half * 4 + kq
                            pz, Bpz = inproj(slab, Bslab, kq * 128)
                            P.act(zs[:, k, :], pz[:], AF.Silu, reads=[Bpz], writes=[Bzs[k]])
                    for q4 in range(4):
                        slab, Bslab = load_slab(w_in, 1024 + q4 * 512, 512)
                        for jq in range(4):
                            j = q4 * 4 + jq
                            pj, Bpj = inproj(slab, Bslab, jq * 128)
                            xp, Bxp = xpre[j % 2], Bxpre[j % 2]
                            ca, Bca = cacc[j % 2], Bcacc[j % 2]
                            P.copy("pool", xp[:, 0:3], halo[:, j, :], reads=[Bhalo[j]], writes=[Bxp])
                            P.copy("act", xp[:, 3:515], pj[:], reads=[Bpj], writes=[Bxp])
                            P.copy("pool", halo[:, j, :], xp[:, 512:515], reads=[Bxp], writes=[Bhalo[j]])
                            P.ts("dve", ca[:], xp[:, 0:512], convw(j, 0), convb(j), ALU.mult, ALU.add,
                                 reads=[Bxp, Bsmall], writes=[Bca])
                            for tap in range(1, 4):
                                P.stt(ca[:], xp[:, tap:tap + 512], convw(j, tap), ca[:], ALU.mult, ALU.add,
                                      reads=[Bxp, Bsmall, Bca], writes=[Bca])
                            if j < 8:
                                P.act(xsT[:, j, :], ca[:], AF.Silu, reads=[Bca], writes=[BxsT[j]])
                            else:
                                P.act(BCT[:, j - 8, :], ca[:], AF.Silu, reads=[Bca], writes=[BBCT[j - 8]])
                    for c in range(4):
                        csl = slice(c * 128, (c + 1) * 128)
                        for q, (src, Bsrc) in enumerate(((dtT, BdtT), (acumT, BacumT), (sT, BsT))):
                            P.mm(ps[2][:, q * 16:(q + 1) * 16], src[:, csl], self.ident32[0:16, 0:16], True, True,
                                 reads=[Bsrc, self.Bcst], writes=[Bps[2]])
                        P.copy("act", tok[:, c, :], ps[2][:, 0:48], reads=[Bps[2]], writes=[Btok[c]])
                        for k in range(8):
                            P.transpose(psT[:, k * 128:(k + 1) * 128], xsT[:, k, csl], self.identb[:],
                                        reads=[BxsT[k], self.Bidentb], writes=[Bps[3]])
                        P.copy("act", xs_tok[:], psT, reads=[Bps[3]], writes=[Bxs_tok])
                        for g in range(4):
                            P.transpose(psT2[:, g * 128:(g + 1) * 128], BCT[:, g, csl], self.identb[:],
                                        reads=[BBCT[g], self.Bidentb], writes=[Bps[2]])
                        P.copy("act", B_tok[:], psT2[:, 0:512], reads=[Bps[2]], writes=[BB_tok])
                        P.tt("pool", V(xdte[:], 16), V(xs_tok[:], 16), tok[:, c, 32:48].unsqueeze(2).broadcast_to([128, 16, 64]),
                             ALU.mult, reads=[Bxs_tok, Btok[c]], writes=[Bxdte])
                        for g in range(4):
                            P.mm(ps[4][:, g * 128:(g + 1) * 128], BCT[:, g, csl], BCT[:, 4 + g, csl], True, True,
                                 reads=[BBCT[g], BBCT[4 + g]], writes=[Bps[4]])
                        P.tt("dve", smask[:], V(ps[4][:], 4), self.maskT.unsqueeze(1).broadcast_to([128, 4, 128]), ALU.mult,
                             reads=[Bps[4], self.Bcst], writes=[Bsmask])
                        if DEBUG and t == 0 and c == 1:
                            P.dma(self.dbg["smask"], smask[:].rearrange("p a b -> p (a b)"), reads=[Bsmask])
                            P.dma(self.dbg["xstok"], xs_tok[:], reads=[Bxs_tok], eng="pool")
                            P.dma(self.dbg["btok"], B_tok[:], reads=[BB_tok], eng="pool")
                            P.dma(self.dbg["xdte"], xdte[:], reads=[Bxdte], eng="pool")
                        for h in range(16):
                            g, k, hh, i = h // 4, h // 2, h % 2, h % 2
                            ar, Bar = ps[5][:, (h % 4) * 128:(h % 4 + 1) * 128], Barow[h % 4]
                            P.mm(ar, sel[:, h, :], acumT[:, csl], True, True, reads=[Bsel, BacumT], writes=[Bar])
                            P.ts("dve", dif[i][:], ar, tok[:, c, 16 + h:17 + h], 0.0, ALU.subtract, ALU.min,
                                 reads=[Bar, Btok[c]], writes=[Bdif[i]])
                            P.act(dif[i][:], dif[i][:], AF.Exp, reads=[Bdif[i]], writes=[Bdif[i]])
                            P.stt(MT[i][:], dif[i][:], tok[:, c, h:h + 1], smask[:, g, :], ALU.mult, ALU.mult,
                                  reads=[Bdif[i], Btok[c], Bsmask], writes=[BMT[i]])
                            P.act(Er[i][:], ar, AF.Exp, reads=[Bar], writes=[BEr[i]])
                            P.tt("pool", Ch[i][:], BCT[:, 4 + g, csl], Er[i][:], ALU.mult, reads=[BBCT[4 + g], BEr[i]], writes=[BCh[i]])
                            if DEBUG and t == 0 and c == 1:
                                P.dma(self.dbg["E"][:, h, :], dif[i][:], reads=[Bdif[i]])
                                P.dma(self.dbg["MT"][:, h, :], MT[i][:], reads=[BMT[i]], eng="pool")
                                P.dma(self.dbg["Er"][:, h, :], Er[i][:], reads=[BEr[i]])
                                P.dma(self.dbg["Ch"][:, h, :], Ch[i][:], reads=[BCh[i]], eng="pool")
                            py = ps[6 + k // 4]
                            Bpy = Bps[6 + k // 4]
                            ksl = slice((k % 4) * 128, (k % 4 + 1) * 128)
                            if hh == 0:
                                P.mm(py[:, ksl], diagD[:, k, :], xsT[:, k, csl], True, False,
                                     reads=[BdiagD, BxsT[k]], writes=[Bpy])
                            kw = {} if hh == 0 else {"tile_position": (0, 64)}
                            P.mm(py[64 * hh:64 * hh + 64, ksl], xs_tok[:, h * 64:(h + 1) * 64], MT[i][:], False, False,
                                 reads=[Bxs_tok, BMT[i]], writes=[Bpy], **kw)
                            P.mm(py[64 * hh:64 * hh + 64, ksl], Sbf[:, h * 64:(h + 1) * 64], Ch[i][:], False, hh == 1,
                                 reads=[BSbf, BCh[i]], writes=[Bpy], **kw)
                        P.copy("act", yT[:, 0:4, csl], V(ps[6][:], 4), reads=[Bps[6]], writes=ByT[0:4])
                        P.copy("act", yT[:, 4:8, csl], V(ps[7][:], 4), reads=[Bps[7]], writes=ByT[4:8])
                        for g in range(4):
                            P.mm(ps[g // 2][:, (g % 2) * 256:(g % 2 + 1) * 256], B_tok[:, g * 128:(g + 1) * 128],
                                 xdte[:, g * 256:(g + 1) * 256], True, True, reads=[BB_tok, Bxdte], writes=[Bps[g // 2]])
                        for half in range(2):
                            Sh = S[:, half * 512:(half + 1) * 512]
                            P.tt("dve", V(Sh, 8), V(Sh, 8), cd[:, c, half * 8:(half + 1) * 8].unsqueeze(2).broadcast_to([128, 8, 64]),
                                 ALU.mult, reads=[BS, Bcd], writes=[BS])
                            P.tt("dve", Sh, Sh, ps[half][:], ALU.add, reads=[BS, Bps[half]], writes=[BS])
                        P.copy("act", Sbf[:], S[:], reads=[BS], writes=[BSbf])
                    if DEBUG:
                        P.dma(self.dbg["tok"][:, t, :], tok[:].rearrange("p a b -> p (a b)"), reads=Btok)
                        P.dma(self.dbg["cd"][:, t, :], cd[:].rearrange("p a b -> p (a b)"), reads=[Bcd])
                        P.dma(self.dbg["acumT"][:, tsl], acumT[:], reads=[BacumT])
                        P.dma(self.dbg["dtT"][:, tsl], dtT[:], reads=[BdtT])
                        P.dma(self.dbg["y"][:, :, tsl], yT[:], reads=ByT)
                        P.dma(self.dbg["xs"][:, :, tsl], xsT[:], reads=BxsT, eng="pool")
                        P.dma(self.dbg["z"][:, :, tsl], zs[:], reads=Bzs)
                        P.dma(self.dbg["bc"][:, :, tsl], BCT[:], reads=BBCT, eng="pool")
                    for gq in range(4):
                        pr, Bpr = ps[2], Bps[2]
                        for kk in range(2):
                            k = gq * 2 + kk
                            P.tt("dve", yT[:, k, :], yT[:, k, :], zs[:, k, :], ALU.mult, reads=[ByT[k], Bzs[k]], writes=[ByT[k]])
                            sq, Bsq = tmp["sq%d" % kk], Btmp["sq%d" % kk]
                            P.act(sq[:], yT[:, k, :], AF.Square, reads=[ByT[k]], writes=[Bsq])
                            P.mm(pr[:], self.ones32[:], sq[:], kk == 0, kk == 1, reads=[self.Bones32, Bsq], writes=[Bpr])
                        rstd, Br = tmp["rstd"], Btmp["rstd"]
                        P.ts("dve", rstd[:], pr[:], 1.0 / 256.0, LN_EPS, ALU.mult, ALU.add, reads=[Bpr], writes=[Br])
                        P.act(rstd[:], rstd[:], AF.Sqrt, reads=[Br], writes=[Br])
                        P.recip(rstd[:], rstd[:], reads=[Br], writes=[Br])
                        for kk in range(2):
                            k = gq * 2 + kk
                            P.stt(ycatA[:, k, :], yT[:, k, :], normw(k), rstd[:], ALU.mult, ALU.mult,
                                  reads=[ByT[k], Bsmall, Br], writes=[BycatA[k]])
                    if DEBUG:
                        P.dma(self.dbg["cat"][:, :, tsl], ycatA[:], reads=BycatA, eng="pool")
                    self.outproj_tile(0, self.ab_w_out, 16,
                                      lambda cc: ycatA[:, cc, :] if cc < 8 else ycatB[:, cc - 8, tsl],
                                      lambda cc: BycatA[cc] if cc < 8 else BycatB[cc - 8][t],
                                      xt32, Bxt, wos, Bwos, tmp, Btmp, t, wocnt)
                P.dma(self.fin_ssd, S[:], reads=[BS])
                P.dma(self.fin_conv, halo[:], reads=Bhalo)
                P.barrier()
            P.barrier()

    def mixer_cd(self):
        P, nc = self.P, self.nc
        ps, Bps = self.ps, self.Bps
        w_in = self.cd_w_in
        with contextlib.ExitStack() as ms:
            sb = lambda n, shp, dt: self.sb("c_" + n, shp, dt, ms)
            small = sb("small", [128, 18], F32)
            Bsmall = Buf()
            P.dma(small[:], self.cd_small, writes=[Bsmall])
            lbt = sb("lbt", [128, 8], F32)
            Blbt = Buf()
            P.tt("dve", lbt[:, 0:4], small[:, 4:8], small[:, 0:4], ALU.subtract, reads=[Bsmall], writes=[Blbt])
            P.act(lbt[:, 0:4], lbt[:, 0:4], AF.Sigmoid, reads=[Blbt], writes=[Blbt])
            P.ts("dve", lbt[:, 4:8], lbt[:, 0:4], -1.0, 1.0, ALU.mult, ALU.add, reads=[Blbt], writes=[Blbt])
            normw = lambda hd: small[:, 8 + hd:9 + hd]
            nbg = sb("nbg", [128, 2], F32)
            Bnbg = Buf()
            P.ts("dve", nbg[:], small[:, 16:18], -1.0, None, ALU.mult, reads=[Bsmall], writes=[Bnbg])
            wup = sb("wup", [16, 256], F32)
            Bwup = Buf()
            P.dma(wup[:], self.gla_wup, writes=[Bwup])
            rmask = sb("rmask", [128, 512], F32)
            Brmask = Buf()
            P.memset("dve", rmask[:], 1.0, writes=[Brmask])
            P.memset("dve", V(rmask[:], 8)[:, :, 0:1], 0.0, writes=[Brmask])
            S = sb("S", [128, 6, 128], F32)
            BS = [Buf() for _ in range(6)]
            P.dma(S[:], self.init_gla.rearrange("p (a b) -> p a b", a=6), writes=BS)
            slabs = [sb("slab%d" % i, [128, 8, 512], BF16) for i in range(2)]
            Bslabs = [Buf() for _ in range(2)]
            slab_cnt = [0]

            def load_slab(c0, ncols):
                i = slab_cnt[0] % 2
                slab_cnt[0] += 1
                P.dma(slabs[i][:, :, 0:ncols], w_in[:, c0:c0 + ncols].rearrange("(k p) c -> p k c", p=128),
                      writes=[Bslabs[i]], eng="pool")
                return slabs[i], Bslabs[i]

            xb = sb("xb", [128, 8, 512], BF16)
            Bxb = [Buf() for _ in range(8)]
            xt32 = sb("xt32", [128, 8, 512], F32)
            Bxt = [Buf() for _ in range(8)]
            wo = sb("wo", [128, 8, 512], BF16)
            Bwo = Buf()
            wos, Bwos, wocnt = [wo, wo], [Bwo, Bwo], [0]
            ycat = sb("ycat", [128, 8, 512], BF16)
            Bycat = [Buf() for _ in range(8)]
            qT = sb("qT", [128, 6, 512], BF16)
            kT = sb("kT", [128, 6, 512], BF16)
            BqT = [Buf() for _ in range(6)]
            BkT = [Buf() for _ in range(6)]
            gcum = sb("gcum", [128, 6, 512], F32)
            Bgc = [Buf() for _ in range(6)]
            gate = sb("gate", [128, 8, 512], F32)
            Bgate = [Buf() for _ in range(8)]
            v_tok = sb("v_tok", [128, 8, 1024], BF16)
            Bvt = [Buf() for _ in range(8)]
            P.memset("pool", v_tok[64:128, :, :], 0.0, writes=Bvt)
            oT = sb("oT", [128, 8, 512], F32)
            BoT = [Buf() for _ in range(8)]
            lrT = sb("lrT", [16, 512], F32)
            BlrT = Buf()
            ftmp = [sb("ftmp%d" % i, [128, 512], F32) for i in range(2)]
            Bftmp = [Buf() for _ in range(2)]
            gm = sb("gm", [128, 6, 8, 4], F32)
            Bgm = Buf()
            elm = sb("elm", [128, 6, 8], F32)
            Belm = Buf()
            ge = [sb("ge%d" % i, [128, 64], F32) for i in range(2)]
            Bge = [Buf() for _ in range(2)]
            eq = [sb("eq%d" % i, [128, 64], F32) for i in range(2)]
            Beq = [Buf() for _ in range(2)]
            qtl = [sb("qtl%d" % i, [128, 64], BF16) for i in range(2)]
            Bqtl = [Buf() for _ in range(2)]
            qm = [[sb("qm%d_%d" % (i, e), [128, 64], BF16) for e in range(2)] for i in range(2)]
            Bqm = [[Buf() for _ in range(2)] for _ in range(2)]
            for i in range(2):
                for e in range(2):
                    P.memset("pool", qm[i][e][:], 0.0, writes=[Bqm[i][e]])
            ktl = [sb("ktl%d" % i, [128, 64], BF16) for i in range(2)]
            Bktl = [Buf() for _ in range(2)]
            ktok = [sb("ktok%d" % i, [128, 128], BF16) for i in range(2)]
            Bktok = [Buf() for _ in range(2)]
            for i in range(2):
                P.memset("pool", ktok[i][64:128, :], 0.0, writes=[Bktok[i]])
            Sbf = [sb("Sbf%d" % i, [128, 128], BF16) for i in range(2)]
            BSbf = [Buf() for _ in range(2)]
            AT = [sb("AT%d" % i, [128, 64], BF16) for i in range(2)]
            BAT = [Buf() for _ in range(2)]
            for i in range(2):
                P.memset("pool", AT[i][64:128, :], 0.0, writes=[BAT[i]])
            stmp = sb("stmp", [128, 128], F32)
            Bstmp = Buf()
            tmp, Btmp = self.ln_tmp(ms, "c_")
            psT = ps[3][:].bitcast(BF16)
            Bslot2 = [Buf() for _ in range(4)]
            Bslot6 = [Buf() for _ in range(4)]
            pcnt = [0]
            hcnt = [0]

            def inproj(slab, Bslab, coff, M=128):
                i = pcnt[0] % 2
                pcnt[0] += 1
                for k in range(8):
                    P.mm(ps[i][0:M, :], slab[:, k, coff:coff + M], xb[:, k, :], k == 0, k == 7,
                         reads=[Bslab, Bxb[k]], writes=[Bps[i]])
                return ps[i], Bps[i]

            for t in range(4):
                tsl = slice(t * 512, (t + 1) * 512)
                for k in range(8):
                    P.dma(xt32[:, k, :], self.xpark[k * 128:(k + 1) * 128, tsl], reads=[self.Bxpark[k][t]], writes=[Bxt[k]])
                for k in range(8):
                    P.copy(("dve", "act")[k % 2], xb[:, k, :], xt32[:, k, :], reads=[Bxt[k]], writes=[Bxb[k]])
                slab, Bslab = load_slab(0, 512)
                for h in range(4):
                    pq, Bpq = inproj(slab, Bslab, h * 128)
                    P.act(qT[:, h, :], pq[:], AF.Silu, reads=[Bpq], writes=[BqT[h]])
                slab, Bslab = load_slab(512, 512)
                for h in range(4):
                    pf, Bpf = inproj(slab, Bslab, h * 128)
                    ft, Bft = ftmp[h % 2], Bftmp[h % 2]
                    P.act(ft[:], pf[:], AF.Sigmoid, reads=[Bpf], writes=[Bft])
                    P.ts("dve", ft[:], ft[:], lbt[:, 4 + h:5 + h], lbt[:, h:h + 1], ALU.mult, ALU.add, reads=[Bft, Blbt], writes=[Bft])
                    P.ts("dve", kT[:, h, :], ft[:], -1.0, 1.0, ALU.mult, ALU.add, reads=[Bft], writes=[BkT[h]])
                    P.act(ft[:], ft[:], AF.Ln, reads=[Bft], writes=[Bft])
                    P.scan(gcum[:, h, :], rmask[:], ft[:], 0.0, reads=[Brmask, Bft], writes=[Bgc[h]])
                for gi, c0 in enumerate((1536, 3088)):
                    slab, Bslab = load_slab(c0, 512)
                    for h in range(4):
                        pg, Bpg = inproj(slab, Bslab, h * 128)
                        P.act(gate[:, gi * 4 + h, :], pg[:], AF.Silu, reads=[Bpg], writes=[Bgate[gi * 4 + h]])
                slab, Bslab = load_slab(2048, 512)
                for pp in range(2):
                    pq, Bpq = inproj(slab, Bslab, pp * 128)
                    P.ts("dve", qT[:, 4 + pp, :], pq[:], 0.125, None, ALU.mult, reads=[Bpq], writes=[BqT[4 + pp]])
                    pk, Bpk = inproj(slab, Bslab, 256 + pp * 128)
                    P.copy("act", kT[:, 4 + pp, :], pk[:], reads=[Bpk], writes=[BkT[4 + pp]])
                slab, Bslab = load_slab(3072, 16)
                pl, Bpl = inproj(slab, Bslab, 0, M=16)
                P.copy("act", lrT[:], pl[0:16, :], reads=[Bpl], writes=[BlrT])
                for pp in range(2):
                    i = pcnt[0] % 2
                    pcnt[0] += 1
                    P.mm(ps[i][:], wup[:, pp * 128:(pp + 1) * 128], lrT[:], True, True, reads=[Bwup, BlrT], writes=[Bps[i]])
                    ft, Bft = ftmp[pp % 2], Bftmp[pp % 2]
                    P.act(ft[:], ps[i][:], AF.Exp, scale=-1.0, bias=nbg[:, pp:pp + 1], reads=[Bps[i], Bnbg], writes=[Bft])
                    P.act(ft[:], ft[:], AF.Ln, bias=1.0, reads=[Bft], writes=[Bft])
                    P.ts("dve", ft[:], ft[:], -1.0 / 16.0, None, ALU.mult, reads=[Bft], writes=[Bft])
                    P.scan(gcum[:, 4 + pp, :], rmask[:], ft[:], 0.0, reads=[Brmask, Bft], writes=[Bgc[4 + pp]])
                s_hi, Bs_hi = load_slab(1024, 512)
                s_gv, Bs_gv = load_slab(2560, 512)
                for c in range(8):
                    csl = slice(c * 64, (c + 1) * 64)
                    for vi, (sl_, Bsl_) in enumerate(((s_hi, Bs_hi), (s_gv, Bs_gv))):
                        i = pcnt[0] % 2
                        pcnt[0] += 1
                        for k in range(8):
                            P.mm(ps[i][0:64, :], xb[:, k, csl], sl_[:, k, :], k == 0, k == 7, reads=[Bxb[k], Bsl_], writes=[Bps[i]])
                        P.copy(("act", "dve")[vi], v_tok[0:64, c, vi * 512:(vi + 1) * 512], ps[i][0:64, :], reads=[Bps[i]], writes=[Bvt[c]])
                g4 = gcum[:].rearrange("p u (c j) -> p u c j", c=8)
                P.copy("dve", gm[:, :, :, 0], g4[:, :, :, 31], reads=Bgc, writes=[Bgm])
                P.copy("dve", gm[:, :, :, 1], g4[:, :, :, 63], reads=Bgc, writes=[Bgm])
                P.act(gm[:, :, :, 2], gm[:, :, :, 0], AF.Exp, reads=[Bgm], writes=[Bgm])
                P.act(gm[:, :, :, 3], gm[:, :, :, 1], AF.Exp, reads=[Bgm], writes=[Bgm])
                P.tt("dve", elm[:], gm[:, :, :, 1], gm[:, :, :, 0], ALU.subtract, reads=[Bgm], writes=[Belm])
                P.act(elm[:], elm[:], AF.Exp, reads=[Belm], writes=[Belm])
                for c in range(8):
                    csl = slice(c * 64, (c + 1) * 64)
                    for u in range(6):
                        i = hcnt[0] % 2
                        hcnt[0] += 1
                        P.ts("dve", ge[i][:], gcum[:, u, csl], gm[:, u, c, 0:1], None, ALU.subtract, reads=[Bgc[u], Bgm], writes=[Bge[i]])
                        P.act(eq[i][:], ge[i][:], AF.Exp, reads=[Bge[i]], writes=[Beq[i]])
                        if u < 4:
                            P.tt("dve", qtl[i][:], qT[:, u, csl], eq[i][:], ALU.mult, reads=[BqT[u], Beq[i]], writes=[Bqtl[i]])
                        else:
                            for e in range(2):
                                rs = slice(64 * e, 64 * e + 64)
                                P.tt("pool", qm[i][e][rs, :], qT[rs, u, csl], eq[i][rs, :], ALU.mult, reads=[BqT[u], Beq[i]], writes=[Bqm[i][e]])
                        P.act(eq[i][:], ge[i][:], AF.Exp, scale=-1.0, reads=[Bge[i], Bqtl[i], Bqm[i][0], Bqm[i][1]], writes=[Beq[i]])
                        P.tt("dve", ktl[i][:], kT[:, u, csl], eq[i][:], ALU.mult, reads=[BkT[u], Beq[i]], writes=[Bktl[i]])
                        P.transpose(psT[0:64, 0:128], ktl[i][:], self.identb[:], reads=[Bktl[i], self.Bidentb], writes=[Bps[3]])
                        P.copy("act", ktok[i][0:64, :], psT[0:64, 0:128], reads=[Bps[3]], writes=[Bktok[i]])
                        P.ts("pool", Sbf[i][:], S[:, u, :], gm[:, u, c, 2:3], None, ALU.mult, reads=[BS[u], Bgm], writes=[BSbf[i]])
                        heads = [(u, None)] if u < 4 else [(4 + (u - 4) * 2, 0), (4 + (u - 4) * 2 + 1, 1)]
                        for hd, e in heads:
                            qop, Bqop = (qtl[i], Bqtl[i]) if e is None else (qm[i][e], Bqm[i][e])
                            a = hd % 2
                            sl2 = slice((hd % 4) * 128, (hd % 4 + 1) * 128)
                            sla = slice((hd % 4) * 128, (hd % 4) * 128 + 64)
                            slo = slice(hd * 64, hd * 64 + 64)
                            P.mm(ps[2][0:64, sla], ktl[i][:], qop[:], True, True, reads=[Bktl[i], Bqop], writes=[Bslot2[hd % 4]])
                            P.tt("dve", AT[a][0:64, :], ps[2][0:64, sla], self.maskT[0:64, 0:64], ALU.mult, reads=[Bslot2[hd % 4], self.Bcst], writes=[BAT[a]])
                            po, Bpo = ps[4 + c % 2], Bps[4 + c % 2]
                            P.mm(po[:, slo], v_tok[:, c, hd * 128:(hd + 1) * 128], AT[a][:], True, False, reads=[Bvt[c], BAT[a]], writes=[Bpo])
                            P.mm(po[:, slo], Sbf[i][:], qop[:], False, True, reads=[BSbf[i], Bqop], writes=[Bpo])
                            P.mm(ps[6][:, sl2], ktok[i][:], v_tok[:, c, hd * 128:(hd + 1) * 128], True, True,
                                 reads=[Bktok[i], Bvt[c]], writes=[Bslot6[hd % 4]])
                            rs = slice(0, 128) if e is None else slice(64 * e, 64 * e + 64)
                            P.ts("dve", S[rs, u, :], S[rs, u, :], gm[rs, u, c, 3:4], None, ALU.mult, reads=[BS[u], Bgm, BSbf[i]], writes=[BS[u]])
                            P.stt(S[rs, u, :], ps[6][rs, sl2], elm[rs, u, c:c + 1], S[rs, u, :], ALU.mult, ALU.add,
                                  reads=[Bslot6[hd % 4], Belm, BS[u]], writes=[BS[u]])
                    P.copy("act", oT[:, :, csl], V(ps[4 + c % 2][:], 8), reads=[Bps[4 + c % 2]], writes=BoT)
                for hd in range(8):
                    sq, Bsq = tmp["sq%d" % (hd % 2)], Btmp["sq%d" % (hd % 2)]
                    pr, Bpr = ps[hd % 2], Bps[hd % 2]
                    P.act(sq[:], oT[:, hd, :], AF.Square, reads=[BoT[hd]], writes=[Bsq])
                    P.mm(pr[:], self.ones32[:], sq[:], True, True, reads=[self.Bones32, Bsq], writes=[Bpr])
                    rstd, Br = tmp["xc%d" % (hd % 2)], Btmp["xc%d" % (hd % 2)]
                    P.ts("dve", rstd[:], pr[:], 1.0 / 128.0, LN_EPS, ALU.mult, ALU.add, reads=[Bpr], writes=[Br])
                    P.act(rstd[:], rstd[:], AF.Sqrt, reads=[Br], writes=[Br])
                    P.recip(rstd[:], rstd[:], reads=[Br], writes=[Br])
                    P.stt(oT[:, hd, :], oT[:, hd, :], normw(hd), rstd[:], ALU.mult, ALU.mult, reads=[BoT[hd], Bsmall, Br], writes=[BoT[hd]])
                    P.tt("pool", ycat[:, hd, :], oT[:, hd, :], gate[:, hd, :], ALU.mult, reads=[BoT[hd], Bgate[hd]], writes=[Bycat[hd]])
                self.outproj_tile(1, self.cd_w_out, 8, lambda cc: ycat[:, cc, :], lambda cc: Bycat[cc],
                                  xt32, Bxt, wos, Bwos, tmp, Btmp, t, wocnt)
            P.dma(self.fin_gla, S[:].rearrange("p a b -> p (a b)"), reads=BS)
            P.barrier()

    def sincos(self, ang, osin, ocos, sf, si, Bang, Bout, Bs):
        P = self.P
        for off, dst in ((0.0, osin), (0.25, ocos)):
            P.ts("dve", sf, ang, 1.0 / (2.0 * np.pi), off, ALU.mult, ALU.add, reads=[Bang], writes=[Bs])
            P.copy("dve", si, sf, reads=[Bs], writes=[Bs])
            P.tt("dve", sf, sf, si, ALU.subtract, reads=[Bs], writes=[Bs])
            P.act(dst, sf, AF.Sin, scale=float(2.0 * np.pi), reads=[Bs], writes=[Bout])

    def s5_phase(self, ms, sb, small, Bsmall, s5d, bglu, ycatB, BycatB, load_slab, xb, Bxb):
        P = self.P
        ps, Bps = self.ps, self.Bps
        w_in = self.ab_w_in
        with contextlib.ExitStack() as p1:
            sb1 = lambda n, shp, dt, stack=None: sb(n, shp, dt, stack or p1)
            cosT = sb1("cosT", [128, 32, 128], F32)
            sinT = sb1("sinT", [128, 32, 128], F32)
            Btab = Buf()
            s5Bb = sb1("s5Bb", [128, 32, 2, 128], BF16)
            Bs5Bb = Buf()
            P.dma(s5Bb[:], self.s5B, writes=[Bs5Bb], eng="pool")
            Ctab = sb1("Ctab", [128, 32, 3, 32], BF16)
            BCtab = Buf()
            diagD5 = sb1("diagD5", [128, 8, 128], BF16)
            BdiagD5 = Buf()
            for k in range(8):
                P.ts("dve", diagD5[:, k, :], self.ident32, s5d(k), None, ALU.mult, reads=[self.Bcst, Bsmall], writes=[BdiagD5])
            pr = sb1("s5pr", [128, 16, 32], F32)
            Bpr = Buf()
            LR, LI, LDT, DT, TH, R, SN, CS, FR, FI, RR, RI, T0, T1, T2, T3 = [pr[:, i, :] for i in range(16)]
            P.dma(pr[:, 0:3, :], self.s5lam, writes=[Bpr])
            zc = sb1("zc", [128, 64], F32)
            Bzc = [Buf() for _ in range(8)]
            P.dma(zc[:], self.init_s5, writes=Bzc)
            with contextlib.ExitStack() as pp:
                sf = sb1("sc_f", [128, 32, 128], F32, pp)
                si = sb1("sc_i", [128, 32, 128], I32, pp)
                ang = sb1("ang", [128, 32, 128], F32, pp)
                Craw = sb1("Craw", [128, 32, 2, 32], F32, pp)
                ctmp = sb1("ctmp", [128, 32, 2, 32], F32, pp)
                it = sb1("iota", [128, 128], F32, pp)
                Bsc, Bang, BCraw, Bctmp, Bit = Buf(), Buf(), Buf(), Buf(), Buf()
                P.dma(Craw[:], self.s5C, writes=[BCraw])
                P.op("pool", lambda e: e.iota(it[:], pattern=[[1, 128]], base=1, channel_multiplier=0,
                                               allow_small_or_imprecise_dtypes=True), writes=[Bit])
                P.act(DT, LDT, AF.Exp, reads=[Bpr], writes=[Bpr])
                P.tt("dve", TH, LI, DT, ALU.mult, reads=[Bpr], writes=[Bpr])
                P.tt("dve", T0, LR, DT, ALU.mult, reads=[Bpr], writes=[Bpr])
                P.act(R, T0, AF.Exp, reads=[Bpr], writes=[Bpr])
                self.sincos(TH, SN, CS, sf[:, 0, 0:32], si[:, 0, 0:32], Bpr, Bpr, Bsc)
                P.tt("dve", T0, R, CS, ALU.mult, reads=[Bpr], writes=[Bpr])
                P.ts("dve", T0, T0, -1.0, None, ALU.add, reads=[Bpr], writes=[Bpr])
                P.tt("dve", T1, R, SN, ALU.mult, reads=[Bpr], writes=[Bpr])
                P.tt("dve", T2, LR, LR, ALU.mult, reads=[Bpr], writes=[Bpr])
                P.tt("dve", T3, LI, LI, ALU.mult, reads=[Bpr], writes=[Bpr])
                P.tt("dve", T2, T2, T3, ALU.add, reads=[Bpr], writes=[Bpr])
                P.recip(T2, T2, reads=[Bpr], writes=[Bpr])
                P.tt("dve", FR, T0, LR, ALU.mult, reads=[Bpr], writes=[Bpr])
                P.tt("dve", T3, T1, LI, ALU.mult, reads=[Bpr], writes=[Bpr])
                P.tt("dve", FR, FR, T3, ALU.add, reads=[Bpr], writes=[Bpr])
                P.tt("dve", FR, FR, T2, ALU.mult, reads=[Bpr], writes=[Bpr])
                P.tt("dve", FI, T1, LR, ALU.mult, reads=[Bpr], writes=[Bpr])
                P.tt("dve", T3, T0, LI, ALU.mult, reads=[Bpr], writes=[Bpr])
                P.tt("dve", FI, FI, T3, ALU.subtract, reads=[Bpr], writes=[Bpr])
                P.tt("dve", FI, FI, T2, ALU.mult, reads=[Bpr], writes=[Bpr])
                P.ts("dve", T0, TH, 128.0, None, ALU.mult, reads=[Bpr], writes=[Bpr])
                self.sincos(T0, RI, RR, sf[:, 0, 0:32], si[:, 0, 0:32], Bpr, Bpr, Bsc)
                frb = FR.unsqueeze(2).broadcast_to([128, 32, 32])
                fib = FI.unsqueeze(2).broadcast_to([128, 32, 32])
                P.tt("dve", ctmp[:, :, 0, :], Craw[:, :, 0, :], frb, ALU.mult, reads=[BCraw, Bpr], writes=[Bctmp])
                P.tt("dve", ctmp[:, :, 1, :], Craw[:, :, 1, :], fib, ALU.mult, reads=[BCraw, Bpr], writes=[Bctmp])
                P.tt("dve", ctmp[:, :, 0, :], ctmp[:, :, 0, :], ctmp[:, :, 1, :], ALU.subtract, reads=[Bctmp], writes=[Bctmp])
                P.copy("dve", Ctab[:, :, 0, :], ctmp[:, :, 0, :], reads=[Bctmp], writes=[BCtab])
                P.ts("dve", Ctab[:, :, 1, :], ctmp[:, :, 0, :], -1.0, None, ALU.mult, reads=[Bctmp], writes=[BCtab])
                P.tt("dve", ctmp[:, :, 0, :], Craw[:, :, 0, :], fib, ALU.mult, reads=[BCraw, Bpr, BCtab], writes=[Bctmp])
                P.tt("dve", ctmp[:, :, 1, :], Craw[:, :, 1, :], frb, ALU.mult, reads=[BCraw, Bpr], writes=[Bctmp])
                P.tt("dve", ctmp[:, :, 0, :], ctmp[:, :, 0, :], ctmp[:, :, 1, :], ALU.add, reads=[Bctmp], writes=[Bctmp])
                P.ts("dve", Ctab[:, :, 2, :], ctmp[:, :, 0, :], -1.0, None, ALU.mult, reads=[Bctmp], writes=[BCtab])
                P.tt("dve", ang[:], TH.unsqueeze(2).broadcast_to([128, 32, 128]), it[:].unsqueeze(1).broadcast_to([128, 32, 128]),
                     ALU.mult, reads=[Bpr, Bit], writes=[Bang])
                self.sincos(ang[:], sinT[:], cosT[:], sf[:], si[:], Bang, Btab, Bsc)
                P.barrier()
            uT = sb1("uT", [128, 8, 512], BF16)
            BuT = [Buf() for _ in range(8)]
            y5T = sb1("y5T", [128, 8, 512], F32)
            By5 = [Buf() for _ in range(8)]
            gb = sb1("gb", [128, 8, 512], BF16)
            Bgb = [Buf() for _ in range(8)]
            mk = lambda n, dt, cnt=2: ([sb1("%s%d" % (n, i), [128, 4, 128], dt) for i in range(cnt)], [Buf() for _ in range(cnt)])
            a32, Ba32 = mk("a32", F32)
            b32, Bb32 = mk("b32", F32)
            t1, Bt1 = mk("t1", F32, 1)
            t2, Bt2 = mk("t2", F32, 1)
            mre, Bmre = mk("mre", F32)
            mim, Bmim = mk("mim", F32)
            wre, Bwre = mk("wre", F32)
            wim, Bwim = mk("wim", F32)
            Pv = [mk("P%d" % v, BF16) for v in range(4)]
            ct = sb1("ct", [128, 4, 4], F32)
            Bct = Buf()
            gt = sb1("gt", [128, 512], F32)
            Bgt = Buf()
            sg = sb1("sg5", [128, 512], F32)
            Bsg = Buf()
            pcnt = [0]
            it_ = 0
            for t in range(4):
                tsl = slice(t * 512, (t + 1) * 512)
                for k in range(8):
                    P.dma(xb[:, k, :], self.xpark[k * 128:(k + 1) * 128, tsl], reads=[self.Bxpark[k][t]], writes=[Bxb[k]], eng="pool")
                for half in range(2):
                    slab, Bslab = load_slab(w_in, 3088 + half * 512, 512)
                    for kq in range(4):
                        k = half * 4 + kq
                        pb_, Bpb_ = ps[6 + pcnt[0] % 2], Bps[6 + pcnt[0] % 2]
                        pcnt[0] += 1
                        for kk in range(8):
                            P.mm(pb_[:], slab[:, kk, kq * 128:(kq + 1) * 128], xb[:, kk, :], kk == 0, kk == 7,
                                 reads=[Bslab, Bxb[kk]], writes=[Bpb_])
                        P.copy("act", uT[:, k, :], pb_[:], reads=[Bpb_], writes=[BuT[k]])
                for c in range(4):
                    csl = slice(c * 128, (c + 1) * 128)
                    for k in range(8):
                        i = it_ % 2
                        it_ += 1
                        pa, Bpa = ps[2 * i], Bps[2 * i]
                        pbm, Bpbm = ps[2 * i + 1], Bps[2 * i + 1]
                        for jj in range(4):
                            j = 4 * k + jj
                            P.mm(pa[:, jj * 128:(jj + 1) * 128], s5Bb[:, j, 0, :], uT[:, k, csl], True, True,
                                 reads=[Bs5Bb, BuT[k]], writes=[Bpa])
                            P.mm(pbm[:, jj * 128:(jj + 1) * 128], s5Bb[:, j, 1, :], uT[:, k, csl], True, True,
                                 reads=[Bs5Bb, BuT[k]], writes=[Bpbm])
                        P.copy("act", a32[i][:], V(pa[:], 4), reads=[Bpa], writes=[Ba32[i]])
                        P.copy("act", b32[i][:], V(pbm[:], 4), reads=[Bpbm], writes=[Bb32[i]])
                        ck = cosT[:, 4 * k:4 * k + 4, :]
                        sk = sinT[:, 4 * k:4 * k + 4, :]
                        P.tt("dve", t1[0][:], a32[i][:], ck, ALU.mult, reads=[Ba32[i], Btab], writes=[Bt1[0]])
                        P.tt("pool", t2[0][:], b32[i][:], sk, ALU.mult, reads=[Bb32[i], Btab], writes=[Bt2[0]])
                        P.tt("dve", mre[i][:], t1[0][:], t2[0][:], ALU.add, reads=[Bt1[0], Bt2[0]], writes=[Bmre[i]])
                        P.tt("pool", t2[0][:], b32[i][:], ck, ALU.mult, reads=[Bb32[i], Btab], writes=[Bt2[0]])
                        P.tt("dve", t1[0][:], a32[i][:], sk, ALU.mult, reads=[Ba32[i], Btab], writes=[Bt1[0]])
                        P.tt("pool", mim[i][:], t2[0][:], t1[0][:], ALU.subtract, reads=[Bt1[0], Bt2[0]], writes=[Bmim[i]])
                        for jj in range(4):
                            j = 4 * k + jj
                            rb = R[:, j:j + 1].broadcast_to([128, 128])
                            P.scan(wre[i][:, jj, :], rb, mre[i][:, jj, :], zc[:, j:j + 1], reads=[Bpr, Bmre[i], Bzc[k]], writes=[Bwre[i]])
                            P.scan(wim[i][:, jj, :], rb, mim[i][:, jj, :], zc[:, 32 + j:33 + j], reads=[Bpr, Bmim[i], Bzc[k]], writes=[Bwim[i]])
                        we_r, we_i = wre[i][:, :, 127], wim[i][:, :, 127]
                        rr, ri = RR[:, 4 * k:4 * k + 4], RI[:, 4 * k:4 * k + 4]
                        P.tt("pool", ct[:, 0, :], we_r, rr, ALU.mult, reads=[Bwre[i], Bpr], writes=[Bct])
                        P.tt("pool", ct[:, 1, :], we_i, ri, ALU.mult, reads=[Bwim[i], Bpr], writes=[Bct])
                        P.tt("pool", ct[:, 2, :], we_r, ri, ALU.mult, reads=[Bwre[i], Bpr], writes=[Bct])
                        P.tt("pool", ct[:, 3, :], we_i, rr, ALU.mult, reads=[Bwim[i], Bpr], writes=[Bct])
                        P.tt("pool", zc[:, 4 * k:4 * k + 4], ct[:, 0, :], ct[:, 1, :], ALU.subtract, reads=[Bct], writes=[Bzc[k]])
                        P.tt("pool", zc[:, 32 + 4 * k:36 + 4 * k], ct[:, 2, :], ct[:, 3, :], ALU.add, reads=[Bct], writes=[Bzc[k]])
                        P.tt("dve", Pv[0][0][i][:], wre[i][:], ck, ALU.mult, reads=[Bwre[i], Btab], writes=[Pv[0][1][i]])
                        P.tt("pool", Pv[1][0][i][:], wim[i][:], sk, ALU.mult, reads=[Bwim[i], Btab], writes=[Pv[1][1][i]])
                        P.tt("dve", Pv[2][0][i][:], wre[i][:], sk, ALU.mult, reads=[Bwre[i], Btab], writes=[Pv[2][1][i]])
                        P.tt("pool", Pv[3][0][i][:], wim[i][:], ck, ALU.mult, reads=[Bwim[i], Btab], writes=[Pv[3][1][i]])
                        py, Bpy = ps[4 + k // 4], Bps[4 + k // 4]
                        ksl = slice((k % 4) * 128, (k % 4 + 1) * 128)
                        P.mm(py[:, ksl], diagD5[:, k, :], uT[:, k, csl], True, False, reads=[BdiagD5, BuT[k]], writes=[Bpy])
                        for jj in range(4):
                            j = 4 * k + jj
                            kw = {} if jj == 0 else {"tile_position": (0, 32 * jj)}
                            for v, cv in enumerate((0, 1, 2, 2)):
                                P.mm(py[32 * jj:32 * jj + 32, ksl], Ctab[:, j, cv, :], Pv[v][0][i][:, jj, :], False,
                                     jj == 3 and v == 3, reads=[BCtab, Pv[v][1][i]], writes=[Bpy], **kw)
                    P.copy("act", y5T[:, 0:4, csl], V(ps[4][:], 4), reads=[Bps[4]], writes=By5[0:4])
                    P.copy("act", y5T[:, 4:8, csl], V(ps[5][:], 4), reads=[Bps[5]], writes=By5[4:8])
                for k in range(8):
                    yk = y5T[:, k, :]
                    P.tt("dve", gt[:], yk, yk, ALU.mult, reads=[By5[k]], writes=[Bgt])
                    P.ts("dve", gt[:], gt[:], 0.044715, 1.0, ALU.mult, ALU.add, reads=[Bgt], writes=[Bgt])
                    P.tt("dve", gt[:], gt[:], yk, ALU.mult, reads=[Bgt, By5[k]], writes=[Bgt])
                    P.act(gt[:], gt[:], AF.Tanh, scale=0.7978845608028654, reads=[Bgt], writes=[Bgt])
                    P.ts("dve", gt[:], gt[:], 1.0, 0.5, ALU.add, ALU.mult, reads=[Bgt], writes=[Bgt])
                    P.tt("dve", yk, gt[:], yk, ALU.mult, reads=[Bgt, By5[k]], writes=[By5[k]])
                    P.copy("act", gb[:, k, :], yk, reads=[By5[k]], writes=[Bgb[k]])
                for half in range(2):
                    slab, Bslab = load_slab(self.s5_w_glu, half * 512, 512)
                    for kq in range(4):
                        kk = half * 4 + kq
                        pg, Bpg = ps[6 + pcnt[0] % 2], Bps[6 + pcnt[0] % 2]
                        pcnt[0] += 1
                        for k in range(8):
                            P.mm(pg[:], slab[:, k, kq * 128:(kq + 1) * 128], gb[:, k, :], k == 0, k == 7,
                                 reads=[Bslab, Bgb[k]], writes=[Bpg])
                        P.act(sg[:], pg[:], AF.Sigmoid, bias=bglu(kk), reads=[Bpg, Bsmall], writes=[Bsg])
                        P.tt("dve", ycatB[:, kk, tsl], y5T[:, kk, :], sg[:], ALU.mult, reads=[By5[kk], Bsg], writes=[BycatB[kk][t]])
            P.dma(self.fin_s5, zc[:], reads=Bzc)
            P.barrier()

    def build(self):
        self.prologue()
        for seg in self.stages:
            kind = seg[0]
            if kind == "open":
                self.open_x32(self.xT if seg[1] == "in" else self.xpark)
            elif kind == "close":
                self.close_x32(self.out if seg[1] == "out" else self.xpark)
            elif kind == "ffn":
                self.ffn(seg[1], seg[2])
            elif kind == "ple":
                self.ple(seg[1])
            elif kind == "mix_cd":
                self.mixer_cd()
            elif kind == "mix_ab":
                self.mixer_ab(**(seg[1] if len(seg) > 1 else {}))
            elif kind == "copy_out":
                self.open_x32(self.xpark)
                self.close_x32(self.out)
        self.P.emit()
        self.st.close()
        return self.nc


FULL_STAGES = [("open", "in"), ("ffn", 0, 0), ("close", "park"), ("mix_ab",),
               ("open", "park"), ("ffn", 0, 1), ("ple", 0), ("ffn", 1, 0), ("close", "park"), ("mix_cd",),
               ("open", "park"), ("ffn", 1, 1), ("ple", 1), ("close", "out")]


def prep_inputs(inp, inits=None):
    f = lambda n: np.asarray(inp[n], np.float32)
    x, p = f("x"), f("p")
    g = f("ln_g").reshape(DEPTH, 3, 8, 128)
    b = f("ln_b").reshape(DEPTH, 3, 8, 128)
    lngb = np.stack([g, b], axis=2)
    lngb = np.ascontiguousarray(lngb.transpose(4, 0, 1, 2, 3).reshape(128, -1))
    shared = {"lngb": lngb}
    for n in ("ffn_w_gate", "ffn_w_up", "ffn_w_down", "ple_w_gate", "ple_w_proj"):
        shared[n] = np.ascontiguousarray(f(n))
    consts = np.zeros((128, 256), np.float32)
    consts[:, 0:128] = np.eye(128, dtype=np.float32)
    consts[:, 128:256] = np.triu(np.ones((128, 128), np.float32))
    shared["consts"] = consts
    shared["ab_w_in"] = np.ascontiguousarray(f("ab_w_in")[0])
    shared["ab_w_out"] = np.ascontiguousarray(f("ab_w_out")[0])
    shared["s5_w_glu"] = np.ascontiguousarray(f("s5_w_glu")[0])
    small = np.zeros((128, 112), np.float32)
    cw = f("ssd_conv_w")[0]
    small[:, 0:64] = cw.reshape(4, 16, 128).transpose(2, 1, 0).reshape(128, 64)
    small[:, 64:80] = f("ssd_conv_b")[0].reshape(16, 128).T
    small[:, 80:88] = np.repeat(f("ssd_d")[0], 64).reshape(8, 128).T
    small[:, 88:96] = f("ssd_norm_w")[0].reshape(8, 128).T
    small[:, 96:104] = f("s5_d")[0].reshape(8, 128).T
    small[:, 104:112] = f("s5_b_glu")[0].reshape(8, 128).T
    shared["ab_small"] = small
    shared["ssd16"] = np.ascontiguousarray(np.stack([f("ssd_dt_bias")[0], f("ssd_a_log")[0]], axis=1))
    def st_layout(a):
        return np.ascontiguousarray(a.reshape(32, 2, 64).transpose(1, 2, 0).reshape(128, 32))
    lam = np.stack([st_layout(f("s5_lambda_re")[0]), st_layout(f("s5_lambda_im")[0]),
                    st_layout(np.repeat(f("s5_log_dt")[0][:, None], 64, axis=1))], axis=1)
    shared["s5lam"] = np.ascontiguousarray(lam)
    Bre, Bim = f("s5_b_re")[0], f("s5_b_im")[0]
    s5B = np.zeros((128, 32, 2, 128), np.float32)
    Cre, Cim = f("s5_c_re")[0], f("s5_c_im")[0]
    s5C = np.zeros((128, 32, 2, 32), np.float32)
    for gg in range(64):
        j, e = gg // 2, gg % 2
        r0 = (gg % 8) * 16
        s5B[r0:r0 + 16, j, 0, e * 64:(e + 1) * 64] = Bre[gg].T
        s5B[r0:r0 + 16, j, 1, e * 64:(e + 1) * 64] = Bim[gg].T
        s5C[e * 64:(e + 1) * 64, j, 0, e * 16:(e + 1) * 16] = Cre[gg].T
        s5C[e * 64:(e + 1) * 64, j, 1, e * 16:(e + 1) * 16] = Cim[gg].T
    shared["s5B"] = s5B
    shared["s5C"] = s5C
    shared["cd_w_in"] = np.ascontiguousarray(f("cd_w_in")[0])
    shared["cd_w_out"] = np.ascontiguousarray(f("cd_w_out")[0])
    cs = np.zeros((128, 18), np.float32)
    lbl = f("hgrn_lb_logits")
    cs[:, 0:4] = lbl[0].reshape(4, 128).T
    cs[:, 4:8] = lbl[1].reshape(4, 128).T
    cs[:, 8:12] = f("hgrn_norm_w")[0].reshape(4, 128).T
    cs[:, 12:16] = f("gla_norm_w")[0].reshape(4, 128).T
    cs[:, 16:18] = f("gla_b_gate")[0].reshape(2, 128).T
    shared["cd_small"] = cs
    shared["gla_wup"] = np.ascontiguousarray(f("gla_w_gate_up")[0])
    maps = []
    big = [n for n in shared if shared[n].nbytes >= (1 << 20)]
    for c in range(8):
        bi, h = c // 2, c % 2
        m = dict(shared)
        for n in big:
            m[n] = np.concatenate([shared[n].reshape(-1), np.full(16, float(c), np.float32)])
        m["xT"] = np.ascontiguousarray(x[bi, h * T:(h + 1) * T, :].T)
        m["pT"] = np.ascontiguousarray(p[:, bi, h * T:(h + 1) * T, :].transpose(0, 2, 1))
        for n, shp in INIT_SHAPES:
            m[n] = np.zeros(shp, np.float32) if inits is None else inits[c][n]
        maps.append(m)
    return maps


INIT_SHAPES = (("init_ssd", (128, 1024)), ("init_conv", (128, 16, 3)), ("init_s5", (128, 64)), ("init_gla", (128, 768)))


def run(inp, stages, cores=8, trace=False, inits=None, full=False):
    nc = K(stages).build()
    maps = prep_inputs(inp, inits)[:cores]
    if trace:
        res = run_bass_kernel_spmd(nc, maps, core_ids=list(range(cores)), trace=True)
        print("exec_time_ns", res.exec_time_ns)
    else:
        res = run_bass_kernel_spmd(nc, maps, core_ids=list(range(cores)))
    if full:
        return [{k: np.asarray(v) for k, v in r.items()} for r in res.results]
    return [np.asarray(r["outT"]) for r in res.results]


def kernel(**inputs):
    nc = K(FULL_STAGES).build()
    zero = {n: np.zeros(shp, np.float32) for n, shp in INIT_SHAPES}
    inits = [dict(zero) for _ in range(8)]
    maps = prep_inputs(inputs, inits)
    res = None
    for it in range(3):
        res = run_bass_kernel_spmd(nc, maps, core_ids=list(range(8))).results
        if it == 2:
            break
        for c in range(1, 8, 2):
            names = ("ssd", "conv", "s5") if it == 0 else ("gla",)
            for n in names:
                maps[c]["init_" + n] = np.ascontiguousarray(np.asarray(res[c - 1]["fin_" + n], np.float32))
    B = inputs["x"].shape[0]
    y = np.empty((B, 2 * T, D), np.float32)
    for c in range(8):
        y[c // 2, (c % 2) * T:(c % 2 + 1) * T, :] = np.asarray(res[c]["outT"]).T
    return y
```

```python
import contextlib
import numpy as np
import concourse.bass as bass
import concourse.mybir as mybir
from concourse.bass_utils import run_bass_kernel_spmd

F32 = mybir.dt.float32
BF16 = mybir.dt.bfloat16
I32 = mybir.dt.int32
AF = mybir.ActivationFunctionType
ALU = mybir.AluOpType

D = 1024
T = 2048
DFF = 2816
NF = DFF // 128
DEPTH = 2
ALPHA = (2.0 * DEPTH) ** 0.25
LN_EPS = 1e-5
EPS_P = LN_EPS / (ALPHA * ALPHA)

DEBUG = False
N_DMA_SEMS = 48
SAME_ENGINE_SYNC = True
ENGS = ("pe", "dve", "act", "pool", "sp")


class Buf:
    __slots__ = ("name", "lw", "rd", "dma_rd")

    def __init__(self, name=""):
        self.name = name
        self.lw = None
        self.rd = {}
        self.dma_rd = []


class Op:
    __slots__ = ("eng", "fn", "deps", "is_dma", "signal", "sigval", "dslot", "dval", "idx", "gid", "is_cc")

    def __init__(self, eng, fn, is_dma):
        self.eng = eng
        self.fn = fn
        self.deps = []
        self.is_dma = is_dma
        self.signal = False
        self.sigval = 0
        self.dslot = -1
        self.dval = 0
        self.idx = -1
        self.gid = -1
        self.is_cc = False


class Prog:
    def __init__(self, nc):
        self.nc = nc
        self.ops = {e: [] for e in ENGS}
        self.seen = {e: {x: -1 for x in ENGS} for e in ENGS}
        self.dma_slot_last = [None] * N_DMA_SEMS
        self.dma_slot_cnt = [0] * N_DMA_SEMS
        self.dma_rr = 0
        self.nops = 0
        self.dma_seen = {e: set() for e in ENGS}
        self.pending = {e: [] for e in ENGS}

    def barrier(self):
        last = [self.ops[e][-1] for e in ENGS if self.ops[e]]
        last += [o for o in self.dma_slot_last if o is not None]
        for e in ENGS:
            self.pending[e] = list(last)

    def _add_dep(self, op, p):
        if p is None or p is op:
            return
        E = op.eng
        if p.is_dma:
            if p.gid in self.dma_seen[E]:
                return
            self.dma_seen[E].add(p.gid)
            op.deps.append(p)
            return
        if p.eng == E and (E == "pe" or E == "sp" or not SAME_ENGINE_SYNC):
            return
        if self.seen[E][p.eng] >= p.idx:
            return
        self.seen[E][p.eng] = p.idx
        op.deps.append(p)

    def collective(self, kind, src, dst, groups, reads=(), writes=()):
        o = self.op("pool", lambda e: e.collective_compute(kind, ALU.bypass, replica_groups=groups,
                                                           ins=[src], outs=[dst]), reads, writes, dma=True, cc=True)
        return o

    def op(self, eng, fn, reads=(), writes=(), dma=False, cc=False):
        o = Op(eng, fn, dma)
        o.is_cc = cc
        o.idx = len(self.ops[eng])
        o.gid = self.nops
        self.nops += 1
        if self.pending[eng]:
            for p in self.pending[eng]:
                self._add_dep(o, p)
            self.pending[eng] = []
        for b in reads:
            self._add_dep(o, b.lw)
        for b in writes:
            self._add_dep(o, b.lw)
            for r in b.rd.values():
                if r.eng != eng:
                    self._add_dep(o, r)
            for r in b.dma_rd:
                self._add_dep(o, r)
        if cc:
            k = len(self.dma_slot_cnt)
            self.dma_slot_cnt.append(1)
            self.dma_slot_last.append(o)
            o.dslot = k
            o.dval = 1
        elif dma:
            k = self.dma_rr
            self.dma_rr = (self.dma_rr + 1) % N_DMA_SEMS
            self._add_dep(o, self.dma_slot_last[k])
            self.dma_slot_cnt[k] += 16
            o.dslot = k
            o.dval = self.dma_slot_cnt[k]
            self.dma_slot_last[k] = o
        for b in reads:
            if dma:
                b.dma_rd.append(o)
            else:
                b.rd[eng] = o
        for b in writes:
            b.lw = o
            b.rd = {}
            b.dma_rd = []
        self.ops[eng].append(o)
        return o

    def dma(self, out, in_, reads=(), writes=(), eng="sp", **kw):
        return self.op(eng, lambda e: e.dma_start(out=out, in_=in_, **kw), reads, writes, dma=True)

    def mm(self, out, lhsT, rhs, start, stop, reads=(), writes=(), **kw):
        return self.op("pe", lambda e: e.matmul(out, lhsT, rhs, start=start, stop=stop, **kw), reads, writes)

    def transpose(self, out, in_, ident, reads=(), writes=()):
        return self.op("pe", lambda e: e.transpose(out, in_, ident), reads, writes)

    def scan(self, out, data0, data1, initial, reads=(), writes=()):
        return self.op("dve", lambda e: e.tensor_tensor_scan(out=out, data0=data0, data1=data1, initial=initial,
                                                              op0=ALU.mult, op1=ALU.add), reads, writes)

    def recip(self, out, in_, reads=(), writes=()):
        return self.op("dve", lambda e: e.reciprocal(out=out, in_=in_), reads, writes)

    def act(self, out, in_, func, reads=(), writes=(), **kw):
        return self.op("act", lambda e: e.activation(out=out, in_=in_, func=func, **kw), reads, writes)

    def tt(self, eng, out, in0, in1, op, reads=(), writes=()):
        return self.op(eng, lambda e: e.tensor_tensor(out=out, in0=in0, in1=in1, op=op), reads, writes)

    def ts(self, eng, out, in0, s1, s2, op0, op1=None, reads=(), writes=()):
        if op1 is None:
            return self.op(eng, lambda e: e.tensor_scalar(out=out, in0=in0, scalar1=s1, scalar2=None, op0=op0), reads, writes)
        return self.op(eng, lambda e: e.tensor_scalar(out=out, in0=in0, scalar1=s1, scalar2=s2, op0=op0, op1=op1), reads, writes)

    def stt(self, out, in0, scalar, in1, op0, op1, reads=(), writes=()):
        return self.op("dve", lambda e: e.scalar_tensor_tensor(out=out, in0=in0, scalar=scalar, in1=in1, op0=op0, op1=op1), reads, writes)

    def copy(self, eng, out, in_, reads=(), writes=()):
        if eng == "act":
            return self.op(eng, lambda e: e.copy(out=out, in_=in_), reads, writes)
        return self.op(eng, lambda e: e.tensor_copy(out=out, in_=in_), reads, writes)

    def memset(self, eng, ap, val, writes=()):
        return self.op(eng, lambda e: e.memset(ap, val), (), writes)

    def emit(self):
        nc = self.nc
        for e in ENGS:
            for o in self.ops[e]:
                for p in o.deps:
                    if not p.is_dma:
                        p.signal = True
        for e in ENGS:
            c = 0
            for o in self.ops[e]:
                if o.signal and not o.is_dma:
                    c += 1
                    o.sigval = c
        with contextlib.ExitStack() as st:
            esem = {e: st.enter_context(nc.semaphore("s_" + e)) for e in ENGS}
            dsem = [st.enter_context(nc.semaphore("d%d" % k)) for k in range(len(self.dma_slot_cnt))]
            block = st.enter_context(nc.Block())
            engobj = {"pe": "tensor", "dve": "vector", "act": "scalar", "pool": "gpsimd", "sp": "sync"}

            def make(ename):
                def body(eng):
                    for o in self.ops[ename]:
                        for p in o.deps:
                            if p.is_dma:
                                eng.wait_ge(dsem[p.dslot], p.dval)
                            else:
                                eng.wait_ge(esem[p.eng], p.sigval)
                        ins = o.fn(eng)
                        if o.is_cc:
                            ins.then_inc(dsem[o.dslot])
                        elif o.is_dma:
                            ins.then_inc(dsem[o.dslot], 16)
                        elif o.signal:
                            ins.then_inc(esem[ename], 1)
                    if ename == "sp":
                        for k in range(len(self.dma_slot_cnt)):
                            if self.dma_slot_cnt[k]:
                                eng.wait_ge(dsem[k], self.dma_slot_cnt[k])
                return body

            for ename in ENGS:
                getattr(block, engobj[ename])(make(ename))


def V(ap, a):
    return ap.rearrange("p (a b) -> p a b", a=a)


class K:
    def __init__(self, stages):
        self.stages = stages
        nc = self.nc = bass.Bass("TRN2", target_bir_lowering=False)
        self.P = Prog(nc)
        self.st = contextlib.ExitStack()
        di = lambda name, shape: nc.dram_tensor(name, list(shape), F32, kind="ExternalInput").ap()
        do = lambda name, shape: nc.dram_tensor(name, list(shape), F32, kind="ExternalOutput").ap()

        def dip(name, shape):
            n = int(np.prod(shape))
            flat = nc.dram_tensor(name, [n + 16], F32, kind="ExternalInput").ap()
            letters = "abcdefg"[:len(shape)]
            pat = "(%s) -> %s" % (" ".join(letters), " ".join(letters))
            return flat[0:n].rearrange(pat, **{l: int(s) for l, s in zip(letters[1:], shape[1:])})

        self.xT = di("xT", [D, T])
        self.pT = di("pT", [DEPTH, 256, T])
        self.lngb = di("lngb", [128, DEPTH * 3 * 2 * 8])
        self.w_gate = dip("ffn_w_gate", [DEPTH, 2, D, DFF])
        self.w_up = dip("ffn_w_up", [DEPTH, 2, D, DFF])
        self.w_down = dip("ffn_w_down", [DEPTH, 2, DFF, D])
        self.ple_wg = dip("ple_w_gate", [DEPTH, D, D])
        self.ple_wp = dip("ple_w_proj", [DEPTH, 256, D])
        self.consts = di("consts", [128, 256])
        self.ab_w_in = dip("ab_w_in", [D, 4112])
        self.ab_w_out = dip("ab_w_out", [2048, D])
        self.s5_w_glu = dip("s5_w_glu", [D, D])
        self.ab_small = di("ab_small", [128, 112])
        self.ssd16 = di("ssd16", [16, 2])
        self.s5lam = di("s5lam", [128, 3, 32])
        self.s5B = dip("s5B", [128, 32, 2, 128])
        self.s5C = dip("s5C", [128, 32, 2, 32])
        self.cd_w_in = dip("cd_w_in", [D, 3600])
        self.cd_w_out = dip("cd_w_out", [D, D])
        self.cd_small = di("cd_small", [128, 18])
        self.gla_wup = di("gla_wup", [16, 256])
        self.selin = di("sel", [128, 8])
        dint = lambda name, shape: nc.dram_tensor(name, list(shape), F32)
        self.st_ab, self.ga_ab, self.ini_ab = dint("st_ab", [128, 1136]), dint("ga_ab", [1024, 1136]), dint("ini_ab", [128, 1136])
        self.st_cd, self.ga_cd, self.ini_cd = dint("st_cd", [128, 768]), dint("ga_cd", [1024, 768]), dint("ini_cd", [128, 768])
        self.Bst_ab, self.Bini_ab, self.Bst_cd, self.Bini_cd = Buf(), Buf(), Buf(), Buf()
        self.out = do("outT", [D, T])
        self.dbg = {n: do("dbg_" + n, [128, 8, T]) for n in ("y", "xs", "z", "cat", "bc")} if DEBUG else {}
        if DEBUG:
            self.dbg["tok"] = do("dbg_tok", [128, 4, 4 * 48])
            self.dbg["cd"] = do("dbg_cd", [128, 4, 64])
            self.dbg["acumT"] = do("dbg_acumT", [16, T])
            self.dbg["dtT"] = do("dbg_dtT", [16, T])
            self.dbg["smask"] = do("dbg_smask", [128, 512])
            self.dbg["xstok"] = do("dbg_xstok", [128, 1024])
            self.dbg["btok"] = do("dbg_btok", [128, 512])
            self.dbg["xdte"] = do("dbg_xdte", [128, 1024])
            for n in ("E", "MT", "Er", "Ch"):
                self.dbg[n] = do("dbg_" + n, [128, 16, 128])
        self.xpark = nc.dram_tensor("xpark", [D, T], F32, kind="Internal").ap()
        self.Bxpark = [[Buf() for _ in range(4)] for _ in range(8)]
        self.lnp = self.sb("lnp", [128, DEPTH * 3 * 2 * 8], F32)
        self.Blnp = Buf("lnp")
        self.ones32 = self.sb("ones32", [128, 128], F32)
        self.Bones32 = Buf("ones32")
        self.cst = self.sb("cst", [128, 256], F32)
        self.Bcst = Buf("cst")
        self.ident32 = self.cst[:, 0:128]
        self.maskT = self.cst[:, 128:256]
        self.identb = self.sb("identb", [128, 128], BF16)
        self.Bidentb = Buf("identb")
        self.ps = [self.st.enter_context(nc.psum_tensor("ps%d" % i, [128, 512], F32)) for i in range(8)]
        self.Bps = [Buf("ps%d" % i) for i in range(8)]
        self.x32 = None

    def sb(self, name, shape, dt, stack=None):
        self.uid = getattr(self, "uid", 0) + 1
        return (stack or self.st).enter_context(self.nc.sbuf_tensor("%s_%d" % (name, self.uid), list(shape), dt))

    def prologue(self):
        P = self.P
        P.dma(self.lnp[:], self.lngb, writes=[self.Blnp])
        P.dma(self.cst[:], self.consts, writes=[self.Bcst])
        P.memset("dve", self.ones32[:], 1.0, writes=[self.Bones32])
        P.copy("dve", self.identb[:], self.ident32, reads=[self.Bcst], writes=[self.Bidentb])

    def open_x32(self, src):
        P = self.P
        self.xstack = contextlib.ExitStack()
        self.x32 = self.sb("x32", [128, 8, T], F32, self.xstack)
        self.Bx32 = [[Buf("x32_%d_%d" % (k, t)) for t in range(4)] for k in range(8)]
        for t in range(4):
            for k in range(8):
                P.dma(self.x32[:, k, t * 512:(t + 1) * 512], src[k * 128:(k + 1) * 128, t * 512:(t + 1) * 512],
                      reads=([self.Bxpark[k][t]] if src is self.xpark else []), writes=[self.Bx32[k][t]])

    def close_x32(self, dst):
        P = self.P
        for t in range(4):
            for k in range(8):
                P.dma(dst[k * 128:(k + 1) * 128, t * 512:(t + 1) * 512], self.x32[:, k, t * 512:(t + 1) * 512],
                      reads=[self.Bx32[k][t]], writes=([self.Bxpark[k][t]] if dst is self.xpark else []))
        P.barrier()
        self.xstack.close()
        self.x32 = None

    def ln_g(self, l, i, k):
        c = ((l * 3 + i) * 2 + 0) * 8 + k
        return self.lnp[:, c:c + 1]

    def ln_b(self, l, i, k):
        c = ((l * 3 + i) * 2 + 1) * 8 + k
        return self.lnp[:, c:c + 1]

    def layer_norm_tile(self, l, i, xap, Bx, tmp, Btmp):
        P = self.P
        ps1, ps2 = self.ps[6], self.ps[7]
        B1, B2 = self.Bps[6], self.Bps[7]
        for k in range(8):
            P.mm(ps1[:], self.ones32[:], xap(k), k == 0, k == 7,
                 reads=[self.Bones32, Bx(k)], writes=[B1])
        for k in range(8):
            sq, Bsq = tmp["sq%d" % (k % 2)], Btmp["sq%d" % (k % 2)]
            P.act(sq[:], xap(k), AF.Square, reads=[Bx(k)], writes=[Bsq])
            P.mm(ps2[:], self.ones32[:], sq[:], k == 0, k == 7, reads=[self.Bones32, Bsq], writes=[B2])
        mean, Bm = tmp["mean"], Btmp["mean"]
        rstd, Br = tmp["rstd"], Btmp["rstd"]
        P.ts("dve", mean[:], ps1[:], 1.0 / D, None, ALU.mult, reads=[B1], writes=[Bm])
        P.tt("dve", rstd[:], mean[:], mean[:], ALU.mult, reads=[Bm], writes=[Br])
        P.stt(rstd[:], ps2[:], 1.0 / D, rstd[:], ALU.mult, ALU.subtract, reads=[B2, Br], writes=[Br])
        P.ts("dve", rstd[:], rstd[:], EPS_P, None, ALU.add, reads=[Br], writes=[Br])
        P.act(rstd[:], rstd[:], AF.Sqrt, reads=[Br], writes=[Br])
        P.recip(rstd[:], rstd[:], reads=[Br], writes=[Br])
        for k in range(8):
            eng = "dve" if k % 2 == 0 else "pool"
            xc, Bxc = tmp["xc%d" % (k % 2)], Btmp["xc%d" % (k % 2)]
            P.tt(eng, xc[:], xap(k), mean[:], ALU.subtract, reads=[Bx(k), Bm], writes=[Bxc])
            P.tt(eng, xc[:], xc[:], rstd[:], ALU.mult, reads=[Bxc, Br], writes=[Bxc])
            P.act(xap(k), xc[:], AF.Identity, scale=self.ln_g(l, i, k), bias=self.ln_b(l, i, k),
                  reads=[Bxc, self.Blnp], writes=[Bx(k)])

    def ln_tmp(self, ls, pre):
        tmp, Btmp = {}, {}
        for n in ("sq0", "sq1", "mean", "rstd", "xc0", "xc1"):
            tmp[n] = self.sb(pre + n, [128, 512], F32, ls)
            Btmp[n] = Buf()
        return tmp, Btmp

    def ffn(self, l, j):
        P, nc = self.P, self.nc
        ln_i = 0 if j == 0 else 2
        wg = self.w_gate[l, j]
        wu = self.w_up[l, j]
        wd = self.w_down[l, j]
        with contextlib.ExitStack() as ls:
            xb = self.sb("f_xb", [128, 8, 1024], BF16, ls)
            Bxb = [[Buf() for _ in range(2)] for _ in range(8)]
            aT = self.sb("f_aT", [128, NF, 1024], BF16, ls)
            BaT = [[Buf() for _ in range(2)] for _ in range(NF)]
            wgs = [self.sb("f_wg%d" % i, [128, 8, 256], BF16, ls) for i in range(2)]
            wus = [self.sb("f_wu%d" % i, [128, 8, 256], BF16, ls) for i in range(2)]
            Bwgs = [Buf() for _ in range(2)]
            Bwus = [Buf() for _ in range(2)]
            wds = [self.sb("f_wd%d" % i, [128, NF, 512], BF16, ls) for i in range(2)]
            Bwds = [Buf() for _ in range(2)]
            sg = [self.sb("f_sg%d" % i, [128, 512], F32, ls) for i in range(2)]
            Bsg = [Buf() for _ in range(2)]
            tmp, Btmp = self.ln_tmp(ls, "f_")
            pcnt = 0
            for s in range(2):
                for k in range(8):
                    for tt in range(2):
                        t = s * 2 + tt
                        eng = ("dve", "act")[(k * 2 + tt) % 2]
                        P.copy(eng, xb[:, k, tt * 512:(tt + 1) * 512], self.x32[:, k, t * 512:(t + 1) * 512],
                               reads=[self.Bx32[k][t]], writes=[Bxb[k][tt]])
                for f2 in range(NF // 2):
                    bi = f2 % 2
                    c0 = f2 * 256
                    P.dma(wgs[bi][:], wg[:, c0:c0 + 256].rearrange("(k p) c -> p k c", p=128), writes=[Bwgs[bi]], eng="pool")
                    P.dma(wus[bi][:], wu[:, c0:c0 + 256].rearrange("(k p) c -> p k c", p=128), writes=[Bwus[bi]], eng="pool")
                    for fi in range(2):
                        f = f2 * 2 + fi
                        for tt in range(2):
                            pg, Bpg = self.ps[(pcnt % 2) * 2], self.Bps[(pcnt % 2) * 2]
                            pu, Bpu = self.ps[(pcnt % 2) * 2 + 1], self.Bps[(pcnt % 2) * 2 + 1]
                            sgi, Bsgi = sg[pcnt % 2], Bsg[pcnt % 2]
                            pcnt += 1
                            for k in range(8):
                                P.mm(pg[:], wgs[bi][:, k, fi * 128:(fi + 1) * 128], xb[:, k, tt * 512:(tt + 1) * 512],
                                     k == 0, k == 7, reads=[Bwgs[bi], Bxb[k][tt]], writes=[Bpg])
                            for k in range(8):
                                P.mm(pu[:], wus[bi][:, k, fi * 128:(fi + 1) * 128], xb[:, k, tt * 512:(tt + 1) * 512],
                                     k == 0, k == 7, reads=[Bwus[bi], Bxb[k][tt]], writes=[Bpu])
                            P.act(sgi[:], pg[:], AF.Silu, reads=[Bpg], writes=[Bsgi])
                            P.tt("dve", aT[:, f, tt * 512:(tt + 1) * 512], sgi[:], pu[:], ALU.mult,
                                 reads=[Bsgi, Bpu], writes=[BaT[f][tt]])
                for h in range(2):
                    for q in range(2):
                        P.dma(wds[h][:, q * 11:(q + 1) * 11, :],
                              wd[q * 11 * 128:(q + 1) * 11 * 128, h * 512:(h + 1) * 512].rearrange("(f p) c -> p f c", p=128),
                              writes=[Bwds[h]], eng="pool")
                    for dc in range(4):
                        kk = h * 4 + dc
                        for tt in range(2):
                            t = s * 2 + tt
                            py, Bpy = self.ps[4 + (pcnt % 2)], self.Bps[4 + (pcnt % 2)]
                            pcnt += 1
                            for f in range(NF):
                                P.mm(py[:], wds[h][:, f, dc * 128:(dc + 1) * 128], aT[:, f, tt * 512:(tt + 1) * 512],
                                     f == 0, f == NF - 1, reads=[Bwds[h], BaT[f][tt]], writes=[Bpy])
                            xs = self.x32[:, kk, t * 512:(t + 1) * 512]
                            P.stt(xs, py[:], 0.5 / ALPHA, xs, ALU.mult, ALU.add,
                                  reads=[Bpy, self.Bx32[kk][t]], writes=[self.Bx32[kk][t]])
                for tt in range(2):
                    t = s * 2 + tt
                    self.layer_norm_tile(l, ln_i, lambda k, t=t: self.x32[:, k, t * 512:(t + 1) * 512],
                                         lambda k, t=t: self.Bx32[k][t], tmp, Btmp)
            P.barrier()

    def ple(self, l):
        P = self.P
        with contextlib.ExitStack() as ls:
            xb = self.sb("p_xb", [128, 8, 512], BF16, ls)
            Bxb = [Buf() for _ in range(8)]
            pb = self.sb("p_pb", [128, 2, T], BF16, ls)
            Bpb = Buf()
            wg = self.sb("p_wg", [128, 8, D], BF16, ls)
            Bwg = Buf()
            wp = self.sb("p_wp", [128, 2, D], BF16, ls)
            Bwp = Buf()
            sg = [self.sb("p_sg%d" % i, [128, 512], F32, ls) for i in range(2)]
            Bsg = [Buf() for _ in range(2)]
            for q in range(4):
                P.dma(wg[:, q * 2:(q + 1) * 2, :], self.ple_wg[l, q * 256:(q + 1) * 256, :].rearrange("(k p) c -> p k c", p=128),
                      writes=[Bwg], eng="pool")
            P.dma(wp[:], self.ple_wp[l].rearrange("(k p) c -> p k c", p=128), writes=[Bwp], eng="pool")
            P.dma(pb[:], self.pT[l].rearrange("(k p) t -> p k t", p=128), writes=[Bpb], eng="pool")
            pcnt = 0
            for t in range(4):
                sl = slice(t * 512, (t + 1) * 512)
                for k in range(8):
                    eng = ("dve", "act")[k % 2]
                    P.copy(eng, xb[:, k, :], self.x32[:, k, sl], reads=[self.Bx32[k][t]], writes=[Bxb[k]])
                for dc in range(8):
                    pg, Bpg = self.ps[(pcnt % 2) * 2], self.Bps[(pcnt % 2) * 2]
                    pp, Bpp = self.ps[(pcnt % 2) * 2 + 1], self.Bps[(pcnt % 2) * 2 + 1]
                    sgi, Bsgi = sg[pcnt % 2], Bsg[pcnt % 2]
                    pcnt += 1
                    for k in range(8):
                        P.mm(pg[:], wg[:, k, dc * 128:(dc + 1) * 128], xb[:, k, :], k == 0, k == 7,
                             reads=[Bwg, Bxb[k]], writes=[Bpg])
                    for k in range(2):
                        P.mm(pp[:], wp[:, k, dc * 128:(dc + 1) * 128], pb[:, k, sl], k == 0, k == 1,
                             reads=[Bwp, Bpb], writes=[Bpp])
                    P.act(sgi[:], pg[:], AF.Sigmoid, reads=[Bpg], writes=[Bsgi])
                    P.tt("dve", sgi[:], sgi[:], pp[:], ALU.mult, reads=[Bsgi, Bpp], writes=[Bsgi])
                    xs = self.x32[:, dc, sl]
                    P.tt("pool", xs, xs, sgi[:], ALU.add, reads=[self.Bx32[dc][t], Bsgi], writes=[self.Bx32[dc][t]])
            P.barrier()

    def outproj_tile(self, l, w_out, nck, ycat_ap, Bycat, xt32, Bxt, wos, Bwos, tmp, Btmp, t, cnt):
        P = self.P
        for hh in range(2):
            wo, Bwo = wos[cnt[0] % 2], Bwos[cnt[0] % 2]
            cnt[0] += 1
            nq = nck // 8
            for q in range(nq):
                P.dma(wo[:, q * 8:(q + 1) * 8, :],
                      w_out[q * 1024:(q + 1) * 1024, hh * 512:(hh + 1) * 512].rearrange("(c p) n -> p c n", p=128),
                      writes=[Bwo], eng="pool")
            for dc in range(4):
                kk = hh * 4 + dc
                po, Bpo = self.ps[4 + (kk % 2)], self.Bps[4 + (kk % 2)]
                for c in range(nck):
                    P.mm(po[:], wo[:, c, dc * 128:(dc + 1) * 128], ycat_ap(c), c == 0, c == nck - 1,
                         reads=[Bwo, Bycat(c)], writes=[Bpo])
                P.stt(xt32[:, kk, :], po[:], 1.0 / ALPHA, xt32[:, kk, :], ALU.mult, ALU.add,
                      reads=[Bpo, Bxt[kk]], writes=[Bxt[kk]])
        self.layer_norm_tile(l, 1, lambda k: xt32[:, k, :], lambda k: Bxt[k], tmp, Btmp)
        for k in range(8):
            P.dma(self.xpark[k * 128:(k + 1) * 128, t * 512:(t + 1) * 512], xt32[:, k, :], reads=[Bxt[k]],
                  writes=[self.Bxpark[k][t]])

    def exchange(self, st, ga, ini, width, Bst, Bini):
        P = self.P
        Bga = Buf()
        P.collective("AllGather", st.ap().opt(), ga.ap().opt(), [list(range(8))], reads=[Bst], writes=[Bga])
        with contextlib.ExitStack() as ls:
            g = self.sb("x_g", [128, 8, width], F32, ls)
            acc = self.sb("x_acc", [128, width], F32, ls)
            sel = self.sb("x_sel", [128, 8], F32, ls)
            Bg, Bacc, Bsel = Buf(), Buf(), Buf()
            P.dma(sel[:], self.selin, writes=[Bsel])
            gv = ga.ap().rearrange("(r p) f -> p r f", p=128)
            for q in range(4):
                P.dma(g[:, 2 * q:2 * q + 2, :], gv[:, 2 * q:2 * q + 2, :], reads=[Bga], writes=[Bg])
            P.ts("dve", acc[:], g[:, 0, :], sel[:, 0:1], None, ALU.mult, reads=[Bg, Bsel], writes=[Bacc])
            for r in range(1, 8):
                P.stt(acc[:], g[:, r, :], sel[:, r:r + 1], acc[:], ALU.mult, ALU.add, reads=[Bg, Bsel, Bacc], writes=[Bacc])
            P.dma(ini.ap(), acc[:], reads=[Bacc], writes=[Bini])
            P.barrier()

    def mixer_ab(self, do_s5=True, so=False):
        P, nc = self.P, self.nc
        ps, Bps = self.ps, self.Bps
        w_in = self.ab_w_in
        with contextlib.ExitStack() as ms:
            sb = lambda n, shp, dt, stack=None: self.sb("m_" + n, shp, dt, stack or ms)
            small = sb("small", [128, 112], F32)
            Bsmall = Buf()
            P.dma(small[:], self.ab_small, writes=[Bsmall])
            convw = lambda j, tap: small[:, j * 4 + tap:j * 4 + tap + 1]
            convb = lambda j: small[:, 64 + j:65 + j]
            Dexp = lambda k: small[:, 80 + k:81 + k]
            normw = lambda k: small[:, 88 + k:89 + k]
            s5d = lambda k: small[:, 96 + k:97 + k]
            bglu = lambda k: small[:, 104 + k:105 + k]
            ycatB = sb("ycatB", [128, 8, T], BF16)
            BycatB = [[Buf() for _ in range(4)] for _ in range(8)]
            slabs = [sb("slab%d" % i, [128, 8, 512], BF16) for i in range(2)]
            Bslabs = [Buf() for _ in range(2)]
            slab_cnt = [0]

            def load_slab(src, c0, ncols):
                i = slab_cnt[0] % 2
                slab_cnt[0] += 1
                P.dma(slabs[i][:, :, 0:ncols], src[:, c0:c0 + ncols].rearrange("(k p) c -> p k c", p=128),
                      writes=[Bslabs[i]], eng="pool")
                return slabs[i], Bslabs[i]

            xb = sb("xb", [128, 8, 512], BF16)
            Bxb = [Buf() for _ in range(8)]

            if do_s5:
                self.s5_phase(ms, sb, small, Bsmall, s5d, bglu, ycatB, BycatB, load_slab, xb, Bxb, so)
            else:
                for k in range(8):
                    for t in range(4):
                        P.memset("pool", ycatB[:, k, t * 512:(t + 1) * 512], 0.0, writes=[BycatB[k][t]])
            P.barrier()

            with contextlib.ExitStack() as p2:
                sb2 = lambda n, shp, dt: sb(n, shp, dt, p2)
                p16 = sb2("p16", [16, 4], F32)
                Bp16 = Buf()
                P.dma(p16[:, 0:2], self.ssd16, writes=[Bp16])
                P.act(p16[:, 2:3], p16[:, 1:2], AF.Exp, reads=[Bp16], writes=[Bp16])
                P.ts("dve", p16[:, 2:3], p16[:, 2:3], -1.0, None, ALU.mult, reads=[Bp16], writes=[Bp16])
                sel = sb2("sel", [16, 16, 128], F32)
                Bsel = Buf()
                P.copy("dve", sel[:], self.ident32[0:16, 0:16].unsqueeze(2).broadcast_to([16, 16, 128]),
                       reads=[self.Bcst], writes=[Bsel])
                ones16 = sb2("ones16", [16, 128], F32)
                Bones16 = Buf()
                P.memset("dve", ones16[:], 1.0, writes=[Bones16])
                rmask = sb2("rmask", [16, 512], F32)
                Brmask = Buf()
                P.memset("dve", rmask[:], 1.0, writes=[Brmask])
                P.memset("dve", V(rmask[:], 4)[:, :, 0:1], 0.0, writes=[Brmask])
                wdt = sb2("wdt", [128, 8, 16], BF16)
                Bwdt = Buf()
                P.dma(wdt[:], w_in[:, 3072:3088].rearrange("(k p) c -> p k c", p=128), writes=[Bwdt], eng="pool")
                diagD = sb2("diagD", [128, 8, 128], BF16)
                BdiagD = Buf()
                for k in range(8):
                    P.ts("dve", diagD[:, k, :], self.ident32, Dexp(k), None, ALU.mult, reads=[self.Bcst, Bsmall], writes=[BdiagD])
                S = sb2("S", [128, 1024], F32)
                BS = Buf()
                Sbf = sb2("Sbf", [128, 1024], BF16)
                BSbf = Buf()
                if so:
                    P.memset("pool", S[:], 0.0, writes=[BS])
                else:
                    P.dma(S[:], self.ini_ab.ap()[:, 0:1024], reads=[self.Bini_ab], writes=[BS])
                P.copy("act", Sbf[:], S[:], reads=[BS], writes=[BSbf])
                halo = sb2("halo", [128, 16, 3], F32)
                Bhalo = [Buf() for _ in range(16)]
                if so:
                    P.memset("pool", halo[:], 0.0, writes=Bhalo)
                else:
                    P.dma(halo[:], self.ini_ab.ap()[:, 1024:1072].rearrange("p (a b) -> p a b", a=16), reads=[self.Bini_ab], writes=Bhalo)
                ycatA = sb2("ycatA", [128, 8, 512], BF16)
                BycatA = [Buf() for _ in range(8)]
                xt32 = sb2("xt32", [128, 8, 512], F32)
                Bxt = [Buf() for _ in range(8)]
                wos = [sb2("wo%d" % i, [128, 16, 512], BF16) for i in range(1)]
                wos = [wos[0], wos[0]]
                Bwo0 = Buf()
                Bwos = [Bwo0, Bwo0]
                wocnt = [0]
                zs = sb2("zs", [128, 8, 512], F32)
                Bzs = [Buf() for _ in range(8)]
                xsT = sb2("xsT", [128, 8, 512], BF16)
                BxsT = [Buf() for _ in range(8)]
                BCT = sb2("BCT", [128, 8, 512], BF16)
                BBCT = [Buf() for _ in range(8)]
                xs_tok = sb2("xs_tok", [128, 1024], BF16)
                Bxs_tok = Buf()
                B_tok = sb2("B_tok", [128, 512], BF16)
                BB_tok = Buf()
                xdte = sb2("xdte", [128, 1024], BF16)
                Bxdte = Buf()
                yT = sb2("yT", [128, 8, 512], F32)
                ByT = [Buf() for _ in range(8)]
                xpre = [sb2("xpre%d" % i, [128, 515], F32) for i in range(2)]
                Bxpre = [Buf() for _ in range(2)]
                cacc = [sb2("cacc%d" % i, [128, 512], F32) for i in range(2)]
                Bcacc = [Buf() for _ in range(2)]
                smask = sb2("smask", [128, 4, 128], F32)
                Bsmask = Buf()
                dif = [sb2("dif%d" % i, [128, 128], F32) for i in range(2)]
                Bdif = [Buf() for _ in range(2)]
                Er = [sb2("Er%d" % i, [128, 128], F32) for i in range(2)]
                BEr = [Buf() for _ in range(2)]
                MT = [sb2("MT%d" % i, [128, 128], BF16) for i in range(2)]
                BMT = [Buf() for _ in range(2)]
                Ch = [sb2("Ch%d" % i, [128, 128], BF16) for i in range(2)]
                BCh = [Buf() for _ in range(2)]
                dtT = sb2("dtT", [16, 512], F32)
                acumT = sb2("acumT", [16, 512], F32)
                sT = sb2("sT", [16, 512], F32)
                d16 = sb2("d16", [16, 512], F32)
                BdtT, BacumT, BsT, Bd16 = Buf(), Buf(), Buf(), Buf()
                dg16 = sb2("dg16", [16, 4, 16], F32)
                Bdg16 = Buf()
                cd = sb2("cd", [128, 4, 16], F32)
                Bcd = Buf()
                tok = sb2("tok", [128, 4, 48], F32)
                Btok = [Buf() for _ in range(4)]
                Barow = [Buf() for _ in range(4)]
                tmp, Btmp = self.ln_tmp(p2, "m_")
                psT = ps[3][:].bitcast(BF16)
                psT2 = ps[2][:].bitcast(BF16)
                pcnt = [0]

                def inproj(slab, Bslab, coff, M=128):
                    i = pcnt[0] % 2
                    pcnt[0] += 1
                    for k in range(8):
                        P.mm(ps[i][0:M, :], slab[:, k, coff:coff + M], xb[:, k, :], k == 0, k == 7,
                             reads=[Bslab, Bxb[k]], writes=[Bps[i]])
                    return ps[i], Bps[i]

                for t in range(4):
                    tsl = slice(t * 512, (t + 1) * 512)
                    for k in range(8):
                        P.dma(xt32[:, k, :], self.xpark[k * 128:(k + 1) * 128, tsl], reads=[self.Bxpark[k][t]], writes=[Bxt[k]])
                    for k in range(8):
                        P.copy(("dve", "act")[k % 2], xb[:, k, :], xt32[:, k, :], reads=[Bxt[k]], writes=[Bxb[k]])
                    pdt, Bpdt = ps[2], Bps[2]
                    for k in range(8):
                        P.mm(pdt[0:16, :], wdt[:, k, :], xb[:, k, :], k == 0, k == 7, reads=[Bwdt, Bxb[k]], writes=[Bpdt])
                    P.act(d16[:], pdt[0:16, :], AF.Exp, bias=p16[:, 0:1], reads=[Bpdt, Bp16], writes=[Bd16])
                    P.act(dtT[:], d16[:], AF.Ln, bias=1.0, reads=[Bd16], writes=[BdtT])
                    P.ts("dve", d16[:], dtT[:], p16[:, 2:3], None, ALU.mult, reads=[BdtT, Bp16], writes=[Bd16])
                    P.scan(acumT[:], rmask[:], d16[:], 0.0, reads=[Brmask, Bd16], writes=[BacumT])
                    tot_b = V(acumT[:], 4)[:, :, 127:128].broadcast_to([16, 4, 128])
                    P.tt("dve", V(d16[:], 4), tot_b, V(acumT[:], 4), ALU.subtract, reads=[BacumT], writes=[Bd16])
                    P.act(d16[:], d16[:], AF.Exp, reads=[Bd16], writes=[Bd16])
                    P.tt("dve", sT[:], dtT[:], d16[:], ALU.mult, reads=[BdtT, Bd16], writes=[BsT])
                    P.tt("dve", dg16[:], V(acumT[:], 4)[:, :, 127:128].broadcast_to([16, 4, 16]),
                         self.ident32[0:16, 0:16].unsqueeze(1).broadcast_to([16, 4, 16]), ALU.mult,
                         reads=[BacumT, self.Bcst], writes=[Bdg16])
                    P.mm(ps[2][:, 0:64], ones16[:], dg16[:].rearrange("p a b -> p (a b)"), True, True,
                         reads=[Bones16, Bdg16], writes=[Bps[2]])
                    P.act(cd[:].rearrange("p a b -> p (a b)"), ps[2][:, 0:64], AF.Exp, reads=[Bps[2]], writes=[Bcd])
                    for half in range(0 if so else 2):
                        slab, Bslab = load_slab(w_in, half * 512, 512)
                        for kq in range(4):
                            k = half * 4 + kq
                            pz, Bpz = inproj(slab, Bslab, kq * 128)
                            P.act(zs[:, k, :], pz[:], AF.Silu, reads=[Bpz], writes=[Bzs[k]])
                    for q4 in range(4):
                        slab, Bslab = load_slab(w_in, 1024 + q4 * 512, 512)
                        for jq in range(4):
                            j = q4 * 4 + jq
                            pj, Bpj = inproj(slab, Bslab, jq * 128)
                            xp, Bxp = xpre[j % 2], Bxpre[j % 2]
                            ca, Bca = cacc[j % 2], Bcacc[j % 2]
                            P.copy("pool", xp[:, 0:3], halo[:, j, :], reads=[Bhalo[j]], writes=[Bxp])
                            P.copy("act", xp[:, 3:515], pj[:], reads=[Bpj], writes=[Bxp])
                            P.copy("pool", halo[:, j, :], xp[:, 512:515], reads=[Bxp], writes=[Bhalo[j]])
                            P.ts("dve", ca[:], xp[:, 0:512], convw(j, 0), convb(j), ALU.mult, ALU.add,
                                 reads=[Bxp, Bsmall], writes=[Bca])
                            for tap in range(1, 4):
                                P.stt(ca[:], xp[:, tap:tap + 512], convw(j, tap), ca[:], ALU.mult, ALU.add,
                                      reads=[Bxp, Bsmall, Bca], writes=[Bca])
                            if so and j >= 12:
                                continue
                            if j < 8:
                                P.act(xsT[:, j, :], ca[:], AF.Silu, reads=[Bca], writes=[BxsT[j]])
                            else:
                                P.act(BCT[:, j - 8, :], ca[:], AF.Silu, reads=[Bca], writes=[BBCT[j - 8]])
                    for c in range(4):
                        csl = slice(c * 128, (c + 1) * 128)
                        for q, (src, Bsrc) in enumerate(((dtT, BdtT), (acumT, BacumT), (sT, BsT))):
                            P.mm(ps[2][:, q * 16:(q + 1) * 16], src[:, csl], self.ident32[0:16, 0:16], True, True,
                                 reads=[Bsrc, self.Bcst], writes=[Bps[2]])
                        P.copy("act", tok[:, c, :], ps[2][:, 0:48], reads=[Bps[2]], writes=[Btok[c]])
                        for k in range(8):
                            P.transpose(psT[:, k * 128:(k + 1) * 128], xsT[:, k, csl], self.identb[:],
                                        reads=[BxsT[k], self.Bidentb], writes=[Bps[3]])
                        P.copy("act", xs_tok[:], psT, reads=[Bps[3]], writes=[Bxs_tok])
                        for g in range(4):
                            P.transpose(psT2[:, g * 128:(g + 1) * 128], BCT[:, g, csl], self.identb[:],
                                        reads=[BBCT[g], self.Bidentb], writes=[Bps[2]])
                        P.copy("act", B_tok[:], psT2[:, 0:512], reads=[Bps[2]], writes=[BB_tok])
                        P.tt("pool", V(xdte[:], 16), V(xs_tok[:], 16), tok[:, c, 32:48].unsqueeze(2).broadcast_to([128, 16, 64]),
                             ALU.mult, reads=[Bxs_tok, Btok[c]], writes=[Bxdte])
                        for g in range(0 if so else 4):
                            P.mm(ps[4][:, g * 128:(g + 1) * 128], BCT[:, g, csl], BCT[:, 4 + g, csl], True, True,
                                 reads=[BBCT[g], BBCT[4 + g]], writes=[Bps[4]])
                        if not so:
                            P.tt("dve", smask[:], V(ps[4][:], 4), self.maskT.unsqueeze(1).broadcast_to([128, 4, 128]), ALU.mult,
                                 reads=[Bps[4], self.Bcst], writes=[Bsmask])
                        if DEBUG and t == 0 and c == 1:
                            P.dma(self.dbg["smask"], smask[:].rearrange("p a b -> p (a b)"), reads=[Bsmask])
                            P.dma(self.dbg["xstok"], xs_tok[:], reads=[Bxs_tok], eng="pool")
                            P.dma(self.dbg["btok"], B_tok[:], reads=[BB_tok], eng="pool")
                            P.dma(self.dbg["xdte"], xdte[:], reads=[Bxdte], eng="pool")
                        for h in range(0 if so else 16):
                            g, k, hh, i = h // 4, h // 2, h % 2, h % 2
                            ar, Bar = ps[5][:, (h % 4) * 128:(h % 4 + 1) * 128], Barow[h % 4]
                            P.mm(ar, sel[:, h, :], acumT[:, csl], True, True, reads=[Bsel, BacumT], writes=[Bar])
                            P.ts("dve", dif[i][:], ar, tok[:, c, 16 + h:17 + h], 0.0, ALU.subtract, ALU.min,
                                 reads=[Bar, Btok[c]], writes=[Bdif[i]])
                            P.act(dif[i][:], dif[i][:], AF.Exp, reads=[Bdif[i]], writes=[Bdif[i]])
                            P.stt(MT[i][:], dif[i][:], tok[:, c, h:h + 1], smask[:, g, :], ALU.mult, ALU.mult,
                                  reads=[Bdif[i], Btok[c], Bsmask], writes=[BMT[i]])
                            P.act(Er[i][:], ar, AF.Exp, reads=[Bar], writes=[BEr[i]])
                            P.tt("pool", Ch[i][:], BCT[:, 4 + g, csl], Er[i][:], ALU.mult, reads=[BBCT[4 + g], BEr[i]], writes=[BCh[i]])
                            if DEBUG and t == 0 and c == 1:
                                P.dma(self.dbg["E"][:, h, :], dif[i][:], reads=[Bdif[i]])
                                P.dma(self.dbg["MT"][:, h, :], MT[i][:], reads=[BMT[i]], eng="pool")
                                P.dma(self.dbg["Er"][:, h, :], Er[i][:], reads=[BEr[i]])
                                P.dma(self.dbg["Ch"][:, h, :], Ch[i][:], reads=[BCh[i]], eng="pool")
                            py = ps[6 + k // 4]
                            Bpy = Bps[6 + k // 4]
                            ksl = slice((k % 4) * 128, (k % 4 + 1) * 128)
                            if hh == 0:
                                P.mm(py[:, ksl], diagD[:, k, :], xsT[:, k, csl], True, False,
                                     reads=[BdiagD, BxsT[k]], writes=[Bpy])
                            kw = {} if hh == 0 else {"tile_position": (0, 64)}
                            P.mm(py[64 * hh:64 * hh + 64, ksl], xs_tok[:, h * 64:(h + 1) * 64], MT[i][:], False, False,
                                 reads=[Bxs_tok, BMT[i]], writes=[Bpy], **kw)
                            P.mm(py[64 * hh:64 * hh + 64, ksl], Sbf[:, h * 64:(h + 1) * 64], Ch[i][:], False, hh == 1,
                                 reads=[BSbf, BCh[i]], writes=[Bpy], **kw)
                        if not so:
                            P.copy("act", yT[:, 0:4, csl], V(ps[6][:], 4), reads=[Bps[6]], writes=ByT[0:4])
                            P.copy("act", yT[:, 4:8, csl], V(ps[7][:], 4), reads=[Bps[7]], writes=ByT[4:8])
                        for g in range(4):
                            P.mm(ps[g // 2][:, (g % 2) * 256:(g % 2 + 1) * 256], B_tok[:, g * 128:(g + 1) * 128],
                                 xdte[:, g * 256:(g + 1) * 256], True, True, reads=[BB_tok, Bxdte], writes=[Bps[g // 2]])
                        for half in range(2):
                            Sh = S[:, half * 512:(half + 1) * 512]
                            P.tt("dve", V(Sh, 8), V(Sh, 8), cd[:, c, half * 8:(half + 1) * 8].unsqueeze(2).broadcast_to([128, 8, 64]),
                                 ALU.mult, reads=[BS, Bcd], writes=[BS])
                            P.tt("dve", Sh, Sh, ps[half][:], ALU.add, reads=[BS, Bps[half]], writes=[BS])
                        P.copy("act", Sbf[:], S[:], reads=[BS], writes=[BSbf])
                    if DEBUG and not so:
                        P.dma(self.dbg["tok"][:, t, :], tok[:].rearrange("p a b -> p (a b)"), reads=Btok)
                        P.dma(self.dbg["cd"][:, t, :], cd[:].rearrange("p a b -> p (a b)"), reads=[Bcd])
                        P.dma(self.dbg["acumT"][:, tsl], acumT[:], reads=[BacumT])
                        P.dma(self.dbg["dtT"][:, tsl], dtT[:], reads=[BdtT])
                        P.dma(self.dbg["y"][:, :, tsl], yT[:], reads=ByT)
                        P.dma(self.dbg["xs"][:, :, tsl], xsT[:], reads=BxsT, eng="pool")
                        P.dma(self.dbg["z"][:, :, tsl], zs[:], reads=Bzs)
                        P.dma(self.dbg["bc"][:, :, tsl], BCT[:], reads=BBCT, eng="pool")
                    for gq in range(0 if so else 4):
                        pr, Bpr = ps[2], Bps[2]
                        for kk in range(2):
                            k = gq * 2 + kk
                            P.tt("dve", yT[:, k, :], yT[:, k, :], zs[:, k, :], ALU.mult, reads=[ByT[k], Bzs[k]], writes=[ByT[k]])
                            sq, Bsq = tmp["sq%d" % kk], Btmp["sq%d" % kk]
                            P.act(sq[:], yT[:, k, :], AF.Square, reads=[ByT[k]], writes=[Bsq])
                            P.mm(pr[:], self.ones32[:], sq[:], kk == 0, kk == 1, reads=[self.Bones32, Bsq], writes=[Bpr])
                        rstd, Br = tmp["rstd"], Btmp["rstd"]
                        P.ts("dve", rstd[:], pr[:], 1.0 / 256.0, LN_EPS, ALU.mult, ALU.add, reads=[Bpr], writes=[Br])
                        P.act(rstd[:], rstd[:], AF.Sqrt, reads=[Br], writes=[Br])
                        P.recip(rstd[:], rstd[:], reads=[Br], writes=[Br])
                        for kk in range(2):
                            k = gq * 2 + kk
                            P.stt(ycatA[:, k, :], yT[:, k, :], normw(k), rstd[:], ALU.mult, ALU.mult,
                                  reads=[ByT[k], Bsmall, Br], writes=[BycatA[k]])
                    if DEBUG and not so:
                        P.dma(self.dbg["cat"][:, :, tsl], ycatA[:], reads=BycatA, eng="pool")
                    if not so:
                      self.outproj_tile(0, self.ab_w_out, 16,
                                      lambda cc: ycatA[:, cc, :] if cc < 8 else ycatB[:, cc - 8, tsl],
                                      lambda cc: BycatA[cc] if cc < 8 else BycatB[cc - 8][t],
                                      xt32, Bxt, wos, Bwos, tmp, Btmp, t, wocnt)
                if so:
                    P.dma(self.st_ab.ap()[:, 0:1024], S[:], reads=[BS], writes=[self.Bst_ab])
                    P.dma(self.st_ab.ap()[:, 1024:1072].rearrange("p (a b) -> p a b", a=16), halo[:], reads=Bhalo, writes=[self.Bst_ab])
                P.barrier()
            P.barrier()

    def mixer_cd(self, so=False):
        P, nc = self.P, self.nc
        ps, Bps = self.ps, self.Bps
        w_in = self.cd_w_in
        with contextlib.ExitStack() as ms:
            sb = lambda n, shp, dt: self.sb("c_" + n, shp, dt, ms)
            small = sb("small", [128, 18], F32)
            Bsmall = Buf()
            P.dma(small[:], self.cd_small, writes=[Bsmall])
            lbt = sb("lbt", [128, 8], F32)
            Blbt = Buf()
            P.tt("dve", lbt[:, 0:4], small[:, 4:8], small[:, 0:4], ALU.subtract, reads=[Bsmall], writes=[Blbt])
            P.act(lbt[:, 0:4], lbt[:, 0:4], AF.Sigmoid, reads=[Blbt], writes=[Blbt])
            P.ts("dve", lbt[:, 4:8], lbt[:, 0:4], -1.0, 1.0, ALU.mult, ALU.add, reads=[Blbt], writes=[Blbt])
            normw = lambda hd: small[:, 8 + hd:9 + hd]
            nbg = sb("nbg", [128, 2], F32)
            Bnbg = Buf()
            P.ts("dve", nbg[:], small[:, 16:18], -1.0, None, ALU.mult, reads=[Bsmall], writes=[Bnbg])
            wup = sb("wup", [16, 256], F32)
            Bwup = Buf()
            P.dma(wup[:], self.gla_wup, writes=[Bwup])
            rmask = sb("rmask", [128, 512], F32)
            Brmask = Buf()
            P.memset("dve", rmask[:], 1.0, writes=[Brmask])
            P.memset("dve", V(rmask[:], 8)[:, :, 0:1], 0.0, writes=[Brmask])
            S = sb("S", [128, 6, 128], F32)
            BS = [Buf() for _ in range(6)]
            if so:
                P.memset("pool", S[:], 0.0, writes=BS)
            else:
                P.dma(S[:], self.ini_cd.ap().rearrange("p (a b) -> p a b", a=6), reads=[self.Bini_cd], writes=BS)
            slabs = [sb("slab%d" % i, [128, 8, 512], BF16) for i in range(2)]
            Bslabs = [Buf() for _ in range(2)]
            slab_cnt = [0]

            def load_slab(c0, ncols):
                i = slab_cnt[0] % 2
                slab_cnt[0] += 1
                P.dma(slabs[i][:, :, 0:ncols], w_in[:, c0:c0 + ncols].rearrange("(k p) c -> p k c", p=128),
                      writes=[Bslabs[i]], eng="pool")
                return slabs[i], Bslabs[i]

            xb = sb("xb", [128, 8, 512], BF16)
            Bxb = [Buf() for _ in range(8)]
            xt32 = sb("xt32", [128, 8, 512], F32)
            Bxt = [Buf() for _ in range(8)]
            wo = sb("wo", [128, 8, 512], BF16)
            Bwo = Buf()
            wos, Bwos, wocnt = [wo, wo], [Bwo, Bwo], [0]
            ycat = sb("ycat", [128, 8, 512], BF16)
            Bycat = [Buf() for _ in range(8)]
            qT = sb("qT", [128, 6, 512], BF16)
            kT = sb("kT", [128, 6, 512], BF16)
            BqT = [Buf() for _ in range(6)]
            BkT = [Buf() for _ in range(6)]
            gcum = sb("gcum", [128, 6, 512], F32)
            Bgc = [Buf() for _ in range(6)]
            gate = sb("gate", [128, 8, 512], F32)
            Bgate = [Buf() for _ in range(8)]
            v_tok = sb("v_tok", [128, 8, 1024], BF16)
            Bvt = [Buf() for _ in range(8)]
            P.memset("pool", v_tok[64:128, :, :], 0.0, writes=Bvt)
            oT = sb("oT", [128, 8, 512], F32)
            BoT = [Buf() for _ in range(8)]
            lrT = sb("lrT", [16, 512], F32)
            BlrT = Buf()
            ftmp = [sb("ftmp%d" % i, [128, 512], F32) for i in range(2)]
            Bftmp = [Buf() for _ in range(2)]
            gm = sb("gm", [128, 6, 8, 4], F32)
            Bgm = Buf()
            elm = sb("elm", [128, 6, 8], F32)
            Belm = Buf()
            ge = [sb("ge%d" % i, [128, 64], F32) for i in range(2)]
            Bge = [Buf() for _ in range(2)]
            eq = [sb("eq%d" % i, [128, 64], F32) for i in range(2)]
            Beq = [Buf() for _ in range(2)]
            qtl = [sb("qtl%d" % i, [128, 64], BF16) for i in range(2)]
            Bqtl = [Buf() for _ in range(2)]
            qm = [[sb("qm%d_%d" % (i, e), [128, 64], BF16) for e in range(2)] for i in range(2)]
            Bqm = [[Buf() for _ in range(2)] for _ in range(2)]
            for i in range(2):
                for e in range(2):
                    P.memset("pool", qm[i][e][:], 0.0, writes=[Bqm[i][e]])
            ktl = [sb("ktl%d" % i, [128, 64], BF16) for i in range(2)]
            Bktl = [Buf() for _ in range(2)]
            ktok = [sb("ktok%d" % i, [128, 128], BF16) for i in range(2)]
            Bktok = [Buf() for _ in range(2)]
            for i in range(2):
                P.memset("pool", ktok[i][64:128, :], 0.0, writes=[Bktok[i]])
            Sbf = [sb("Sbf%d" % i, [128, 128], BF16) for i in range(2)]
            BSbf = [Buf() for _ in range(2)]
            AT = [sb("AT%d" % i, [128, 64], BF16) for i in range(2)]
            BAT = [Buf() for _ in range(2)]
            for i in range(2):
                P.memset("pool", AT[i][64:128, :], 0.0, writes=[BAT[i]])
            stmp = sb("stmp", [128, 128], F32)
            Bstmp = Buf()
            tmp, Btmp = self.ln_tmp(ms, "c_")
            psT = ps[3][:].bitcast(BF16)
            Bslot2 = [Buf() for _ in range(4)]
            Bslot6 = [Buf() for _ in range(4)]
            pcnt = [0]
            hcnt = [0]

            def inproj(slab, Bslab, coff, M=128):
                i = pcnt[0] % 2
                pcnt[0] += 1
                for k in range(8):
                    P.mm(ps[i][0:M, :], slab[:, k, coff:coff + M], xb[:, k, :], k == 0, k == 7,
                         reads=[Bslab, Bxb[k]], writes=[Bps[i]])
                return ps[i], Bps[i]

            for t in range(4):
                tsl = slice(t * 512, (t + 1) * 512)
                for k in range(8):
                    P.dma(xt32[:, k, :], self.xpark[k * 128:(k + 1) * 128, tsl], reads=[self.Bxpark[k][t]], writes=[Bxt[k]])
                for k in range(8):
                    P.copy(("dve", "act")[k % 2], xb[:, k, :], xt32[:, k, :], reads=[Bxt[k]], writes=[Bxb[k]])
                if not so:
                    slab, Bslab = load_slab(0, 512)
                for h in range(0 if so else 4):
                    pq, Bpq = inproj(slab, Bslab, h * 128)
                    P.act(qT[:, h, :], pq[:], AF.Silu, reads=[Bpq], writes=[BqT[h]])
                slab, Bslab = load_slab(512, 512)
                for h in range(4):
                    pf, Bpf = inproj(slab, Bslab, h * 128)
                    ft, Bft = ftmp[h % 2], Bftmp[h % 2]
                    P.act(ft[:], pf[:], AF.Sigmoid, reads=[Bpf], writes=[Bft])
                    P.ts("dve", ft[:], ft[:], lbt[:, 4 + h:5 + h], lbt[:, h:h + 1], ALU.mult, ALU.add, reads=[Bft, Blbt], writes=[Bft])
                    P.ts("dve", kT[:, h, :], ft[:], -1.0, 1.0, ALU.mult, ALU.add, reads=[Bft], writes=[BkT[h]])
                    P.act(ft[:], ft[:], AF.Ln, reads=[Bft], writes=[Bft])
                    P.scan(gcum[:, h, :], rmask[:], ft[:], 0.0, reads=[Brmask, Bft], writes=[Bgc[h]])
                for gi, c0 in enumerate(() if so else (1536, 3088)):
                    slab, Bslab = load_slab(c0, 512)
                    for h in range(4):
                        pg, Bpg = inproj(slab, Bslab, h * 128)
                        P.act(gate[:, gi * 4 + h, :], pg[:], AF.Silu, reads=[Bpg], writes=[Bgate[gi * 4 + h]])
                slab, Bslab = load_slab(2048, 512)
                for pp in range(2):
                    if not so:
                        pq, Bpq = inproj(slab, Bslab, pp * 128)
                        P.ts("dve", qT[:, 4 + pp, :], pq[:], 0.125, None, ALU.mult, reads=[Bpq], writes=[BqT[4 + pp]])
                    pk, Bpk = inproj(slab, Bslab, 256 + pp * 128)
                    P.copy("act", kT[:, 4 + pp, :], pk[:], reads=[Bpk], writes=[BkT[4 + pp]])
                slab, Bslab = load_slab(3072, 16)
                pl, Bpl = inproj(slab, Bslab, 0, M=16)
                P.copy("act", lrT[:], pl[0:16, :], reads=[Bpl], writes=[BlrT])
                for pp in range(2):
                    i = pcnt[0] % 2
                    pcnt[0] += 1
                    P.mm(ps[i][:], wup[:, pp * 128:(pp + 1) * 128], lrT[:], True, True, reads=[Bwup, BlrT], writes=[Bps[i]])
                    ft, Bft = ftmp[pp % 2], Bftmp[pp % 2]
                    P.act(ft[:], ps[i][:], AF.Exp, scale=-1.0, bias=nbg[:, pp:pp + 1], reads=[Bps[i], Bnbg], writes=[Bft])
                    P.act(ft[:], ft[:], AF.Ln, bias=1.0, reads=[Bft], writes=[Bft])
                    P.ts("dve", ft[:], ft[:], -1.0 / 16.0, None, ALU.mult, reads=[Bft], writes=[Bft])
                    P.scan(gcum[:, 4 + pp, :], rmask[:], ft[:], 0.0, reads=[Brmask, Bft], writes=[Bgc[4 + pp]])
                s_hi, Bs_hi = load_slab(1024, 512)
                s_gv, Bs_gv = load_slab(2560, 512)
                for c in range(8):
                    csl = slice(c * 64, (c + 1) * 64)
                    for vi, (sl_, Bsl_) in enumerate(((s_hi, Bs_hi), (s_gv, Bs_gv))):
                        i = pcnt[0] % 2
                        pcnt[0] += 1
                        for k in range(8):
                            P.mm(ps[i][0:64, :], xb[:, k, csl], sl_[:, k, :], k == 0, k == 7, reads=[Bxb[k], Bsl_], writes=[Bps[i]])
                        P.copy(("act", "dve")[vi], v_tok[0:64, c, vi * 512:(vi + 1) * 512], ps[i][0:64, :], reads=[Bps[i]], writes=[Bvt[c]])
                g4 = gcum[:].rearrange("p u (c j) -> p u c j", c=8)
                P.copy("dve", gm[:, :, :, 0], g4[:, :, :, 31], reads=Bgc, writes=[Bgm])
                P.copy("dve", gm[:, :, :, 1], g4[:, :, :, 63], reads=Bgc, writes=[Bgm])
                P.act(gm[:, :, :, 2], gm[:, :, :, 0], AF.Exp, reads=[Bgm], writes=[Bgm])
                P.act(gm[:, :, :, 3], gm[:, :, :, 1], AF.Exp, reads=[Bgm], writes=[Bgm])
                P.tt("dve", elm[:], gm[:, :, :, 1], gm[:, :, :, 0], ALU.subtract, reads=[Bgm], writes=[Belm])
                P.act(elm[:], elm[:], AF.Exp, reads=[Belm], writes=[Belm])
                for c in range(8):
                    csl = slice(c * 64, (c + 1) * 64)
                    for u in range(6):
                        i = hcnt[0] % 2
                        hcnt[0] += 1
                        P.ts("dve", ge[i][:], gcum[:, u, csl], gm[:, u, c, 0:1], None, ALU.subtract, reads=[Bgc[u], Bgm], writes=[Bge[i]])
                        if so:
                            pass
                        elif u < 4:
                            P.act(eq[i][:], ge[i][:], AF.Exp, reads=[Bge[i]], writes=[Beq[i]])
                            P.tt("dve", qtl[i][:], qT[:, u, csl], eq[i][:], ALU.mult, reads=[BqT[u], Beq[i]], writes=[Bqtl[i]])
                        else:
                            P.act(eq[i][:], ge[i][:], AF.Exp, reads=[Bge[i]], writes=[Beq[i]])
                            for e in range(2):
                                rs = slice(64 * e, 64 * e + 64)
                                P.tt("pool", qm[i][e][rs, :], qT[rs, u, csl], eq[i][rs, :], ALU.mult, reads=[BqT[u], Beq[i]], writes=[Bqm[i][e]])
                        P.act(eq[i][:], ge[i][:], AF.Exp, scale=-1.0, reads=[Bge[i], Bqtl[i], Bqm[i][0], Bqm[i][1]], writes=[Beq[i]])
                        P.tt("dve", ktl[i][:], kT[:, u, csl], eq[i][:], ALU.mult, reads=[BkT[u], Beq[i]], writes=[Bktl[i]])
                        P.transpose(psT[0:64, 0:128], ktl[i][:], self.identb[:], reads=[Bktl[i], self.Bidentb], writes=[Bps[3]])
                        P.copy("act", ktok[i][0:64, :], psT[0:64, 0:128], reads=[Bps[3]], writes=[Bktok[i]])
                        if not so:
                            P.ts("pool", Sbf[i][:], S[:, u, :], gm[:, u, c, 2:3], None, ALU.mult, reads=[BS[u], Bgm], writes=[BSbf[i]])
                        heads = [(u, None)] if u < 4 else [(4 + (u - 4) * 2, 0), (4 + (u - 4) * 2 + 1, 1)]
                        for hd, e in heads:
                            qop, Bqop = (qtl[i], Bqtl[i]) if e is None else (qm[i][e], Bqm[i][e])
                            a = hd % 2
                            sl2 = slice((hd % 4) * 128, (hd % 4 + 1) * 128)
                            sla = slice((hd % 4) * 128, (hd % 4) * 128 + 64)
                            slo = slice(hd * 64, hd * 64 + 64)
                            if not so:
                                P.mm(ps[2][0:64, sla], ktl[i][:], qop[:], True, True, reads=[Bktl[i], Bqop], writes=[Bslot2[hd % 4]])
                                P.tt("dve", AT[a][0:64, :], ps[2][0:64, sla], self.maskT[0:64, 0:64], ALU.mult, reads=[Bslot2[hd % 4], self.Bcst], writes=[BAT[a]])
                                po, Bpo = ps[4 + c % 2], Bps[4 + c % 2]
                                P.mm(po[:, slo], v_tok[:, c, hd * 128:(hd + 1) * 128], AT[a][:], True, False, reads=[Bvt[c], BAT[a]], writes=[Bpo])
                                P.mm(po[:, slo], Sbf[i][:], qop[:], False, True, reads=[BSbf[i], Bqop], writes=[Bpo])
                            P.mm(ps[6][:, sl2], ktok[i][:], v_tok[:, c, hd * 128:(hd + 1) * 128], True, True,
                                 reads=[Bktok[i], Bvt[c]], writes=[Bslot6[hd % 4]])
                            rs = slice(0, 128) if e is None else slice(64 * e, 64 * e + 64)
                            P.ts("dve", S[rs, u, :], S[rs, u, :], gm[rs, u, c, 3:4], None, ALU.mult, reads=[BS[u], Bgm, BSbf[i]], writes=[BS[u]])
                            P.stt(S[rs, u, :], ps[6][rs, sl2], elm[rs, u, c:c + 1], S[rs, u, :], ALU.mult, ALU.add,
                                  reads=[Bslot6[hd % 4], Belm, BS[u]], writes=[BS[u]])
                    if not so:
                        P.copy("act", oT[:, :, csl], V(ps[4 + c % 2][:], 8), reads=[Bps[4 + c % 2]], writes=BoT)
                for hd in range(0 if so else 8):
                    sq, Bsq = tmp["sq%d" % (hd % 2)], Btmp["sq%d" % (hd % 2)]
                    pr, Bpr = ps[hd % 2], Bps[hd % 2]
                    P.act(sq[:], oT[:, hd, :], AF.Square, reads=[BoT[hd]], writes=[Bsq])
                    P.mm(pr[:], self.ones32[:], sq[:], True, True, reads=[self.Bones32, Bsq], writes=[Bpr])
                    rstd, Br = tmp["xc%d" % (hd % 2)], Btmp["xc%d" % (hd % 2)]
                    P.ts("dve", rstd[:], pr[:], 1.0 / 128.0, LN_EPS, ALU.mult, ALU.add, reads=[Bpr], writes=[Br])
                    P.act(rstd[:], rstd[:], AF.Sqrt, reads=[Br], writes=[Br])
                    P.recip(rstd[:], rstd[:], reads=[Br], writes=[Br])
                    P.stt(oT[:, hd, :], oT[:, hd, :], normw(hd), rstd[:], ALU.mult, ALU.mult, reads=[BoT[hd], Bsmall, Br], writes=[BoT[hd]])
                    P.tt("pool", ycat[:, hd, :], oT[:, hd, :], gate[:, hd, :], ALU.mult, reads=[BoT[hd], Bgate[hd]], writes=[Bycat[hd]])
                if not so:
                    self.outproj_tile(1, self.cd_w_out, 8, lambda cc: ycat[:, cc, :], lambda cc: Bycat[cc],
                                      xt32, Bxt, wos, Bwos, tmp, Btmp, t, wocnt)
            if so:
                P.dma(self.st_cd.ap(), S[:].rearrange("p a b -> p (a b)"), reads=BS, writes=[self.Bst_cd])
            P.barrier()

    def sincos(self, ang, osin, ocos, sf, si, Bang, Bout, Bs):
        P = self.P
        for off, dst in ((0.0, osin), (0.25, ocos)):
            P.ts("dve", sf, ang, 1.0 / (2.0 * np.pi), off, ALU.mult, ALU.add, reads=[Bang], writes=[Bs])
            P.copy("dve", si, sf, reads=[Bs], writes=[Bs])
            P.tt("dve", sf, sf, si, ALU.subtract, reads=[Bs], writes=[Bs])
            P.act(dst, sf, AF.Sin, scale=float(2.0 * np.pi), reads=[Bs], writes=[Bout])

    def s5_phase(self, ms, sb, small, Bsmall, s5d, bglu, ycatB, BycatB, load_slab, xb, Bxb, so=False):
        P = self.P
        ps, Bps = self.ps, self.Bps
        w_in = self.ab_w_in
        with contextlib.ExitStack() as p1:
            sb1 = lambda n, shp, dt, stack=None: sb(n, shp, dt, stack or p1)
            cosT = sb1("cosT", [128, 32, 128], F32)
            sinT = sb1("sinT", [128, 32, 128], F32)
            Btab = Buf()
            s5Bb = sb1("s5Bb", [128, 32, 2, 128], BF16)
            Bs5Bb = Buf()
            P.dma(s5Bb[:], self.s5B, writes=[Bs5Bb], eng="pool")
            Ctab = sb1("Ctab", [128, 32, 3, 32], BF16)
            BCtab = Buf()
            diagD5 = sb1("diagD5", [128, 8, 128], BF16)
            BdiagD5 = Buf()
            for k in range(8):
                P.ts("dve", diagD5[:, k, :], self.ident32, s5d(k), None, ALU.mult, reads=[self.Bcst, Bsmall], writes=[BdiagD5])
            pr = sb1("s5pr", [128, 16, 32], F32)
            Bpr = Buf()
            LR, LI, LDT, DT, TH, R, SN, CS, FR, FI, RR, RI, T0, T1, T2, T3 = [pr[:, i, :] for i in range(16)]
            P.dma(pr[:, 0:3, :], self.s5lam, writes=[Bpr])
            zc = sb1("zc", [128, 64], F32)
            Bzc = [Buf() for _ in range(8)]
            if so:
                P.memset("pool", zc[:], 0.0, writes=Bzc)
            else:
                P.dma(zc[:], self.ini_ab.ap()[:, 1072:1136], reads=[self.Bini_ab], writes=Bzc)
            with contextlib.ExitStack() as pp:
                sf = sb1("sc_f", [128, 32, 128], F32, pp)
                si = sb1("sc_i", [128, 32, 128], I32, pp)
                ang = sb1("ang", [128, 32, 128], F32, pp)
                Craw = sb1("Craw", [128, 32, 2, 32], F32, pp)
                ctmp = sb1("ctmp", [128, 32, 2, 32], F32, pp)
                it = sb1("iota", [128, 128], F32, pp)
                Bsc, Bang, BCraw, Bctmp, Bit = Buf(), Buf(), Buf(), Buf(), Buf()
                P.dma(Craw[:], self.s5C, writes=[BCraw])
                P.op("pool", lambda e: e.iota(it[:], pattern=[[1, 128]], base=1, channel_multiplier=0,
                                               allow_small_or_imprecise_dtypes=True), writes=[Bit])
                P.act(DT, LDT, AF.Exp, reads=[Bpr], writes=[Bpr])
                P.tt("dve", TH, LI, DT, ALU.mult, reads=[Bpr], writes=[Bpr])
                P.tt("dve", T0, LR, DT, ALU.mult, reads=[Bpr], writes=[Bpr])
                P.act(R, T0, AF.Exp, reads=[Bpr], writes=[Bpr])
                self.sincos(TH, SN, CS, sf[:, 0, 0:32], si[:, 0, 0:32], Bpr, Bpr, Bsc)
                P.tt("dve", T0, R, CS, ALU.mult, reads=[Bpr], writes=[Bpr])
                P.ts("dve", T0, T0, -1.0, None, ALU.add, reads=[Bpr], writes=[Bpr])
                P.tt("dve", T1, R, SN, ALU.mult, reads=[Bpr], writes=[Bpr])
                P.tt("dve", T2, LR, LR, ALU.mult, reads=[Bpr], writes=[Bpr])
                P.tt("dve", T3, LI, LI, ALU.mult, reads=[Bpr], writes=[Bpr])
                P.tt("dve", T2, T2, T3, ALU.add, reads=[Bpr], writes=[Bpr])
                P.recip(T2, T2, reads=[Bpr], writes=[Bpr])
                P.tt("dve", FR, T0, LR, ALU.mult, reads=[Bpr], writes=[Bpr])
                P.tt("dve", T3, T1, LI, ALU.mult, reads=[Bpr], writes=[Bpr])
                P.tt("dve", FR, FR, T3, ALU.add, reads=[Bpr], writes=[Bpr])
                P.tt("dve", FR, FR, T2, ALU.mult, reads=[Bpr], writes=[Bpr])
                P.tt("dve", FI, T1, LR, ALU.mult, reads=[Bpr], writes=[Bpr])
                P.tt("dve", T3, T0, LI, ALU.mult, reads=[Bpr], writes=[Bpr])
                P.tt("dve", FI, FI, T3, ALU.subtract, reads=[Bpr], writes=[Bpr])
                P.tt("dve", FI, FI, T2, ALU.mult, reads=[Bpr], writes=[Bpr])
                P.ts("dve", T0, TH, 128.0, None, ALU.mult, reads=[Bpr], writes=[Bpr])
                self.sincos(T0, RI, RR, sf[:, 0, 0:32], si[:, 0, 0:32], Bpr, Bpr, Bsc)
                frb = FR.unsqueeze(2).broadcast_to([128, 32, 32])
                fib = FI.unsqueeze(2).broadcast_to([128, 32, 32])
                P.tt("dve", ctmp[:, :, 0, :], Craw[:, :, 0, :], frb, ALU.mult, reads=[BCraw, Bpr], writes=[Bctmp])
                P.tt("dve", ctmp[:, :, 1, :], Craw[:, :, 1, :], fib, ALU.mult, reads=[BCraw, Bpr], writes=[Bctmp])
                P.tt("dve", ctmp[:, :, 0, :], ctmp[:, :, 0, :], ctmp[:, :, 1, :], ALU.subtract, reads=[Bctmp], writes=[Bctmp])
                P.copy("dve", Ctab[:, :, 0, :], ctmp[:, :, 0, :], reads=[Bctmp], writes=[BCtab])
                P.ts("dve", Ctab[:, :, 1, :], ctmp[:, :, 0, :], -1.0, None, ALU.mult, reads=[Bctmp], writes=[BCtab])
                P.tt("dve", ctmp[:, :, 0, :], Craw[:, :, 0, :], fib, ALU.mult, reads=[BCraw, Bpr, BCtab], writes=[Bctmp])
                P.tt("dve", ctmp[:, :, 1, :], Craw[:, :, 1, :], frb, ALU.mult, reads=[BCraw, Bpr], writes=[Bctmp])
                P.tt("dve", ctmp[:, :, 0, :], ctmp[:, :, 0, :], ctmp[:, :, 1, :], ALU.add, reads=[Bctmp], writes=[Bctmp])
                P.ts("dve", Ctab[:, :, 2, :], ctmp[:, :, 0, :], -1.0, None, ALU.mult, reads=[Bctmp], writes=[BCtab])
                P.tt("dve", ang[:], TH.unsqueeze(2).broadcast_to([128, 32, 128]), it[:].unsqueeze(1).broadcast_to([128, 32, 128]),
                     ALU.mult, reads=[Bpr, Bit], writes=[Bang])
                self.sincos(ang[:], sinT[:], cosT[:], sf[:], si[:], Bang, Btab, Bsc)
                P.barrier()
            uT = sb1("uT", [128, 8, 512], BF16)
            BuT = [Buf() for _ in range(8)]
            y5T = sb1("y5T", [128, 8, 512], F32)
            By5 = [Buf() for _ in range(8)]
            gb = sb1("gb", [128, 8, 512], BF16)
            Bgb = [Buf() for _ in range(8)]
            mk = lambda n, dt, cnt=2: ([sb1("%s%d" % (n, i), [128, 4, 128], dt) for i in range(cnt)], [Buf() for _ in range(cnt)])
            a32, Ba32 = mk("a32", F32)
            b32, Bb32 = mk("b32", F32)
            t1, Bt1 = mk("t1", F32, 1)
            t2, Bt2 = mk("t2", F32, 1)
            mre, Bmre = mk("mre", F32)
            mim, Bmim = mk("mim", F32)
            wre, Bwre = mk("wre", F32)
            wim, Bwim = mk("wim", F32)
            Pv = [mk("P%d" % v, BF16) for v in range(4)]
            ct = sb1("ct", [128, 4, 4], F32)
            Bct = Buf()
            gt = sb1("gt", [128, 512], F32)
            Bgt = Buf()
            sg = sb1("sg5", [128, 512], F32)
            Bsg = Buf()
            pcnt = [0]
            it_ = 0
            for t in range(4):
                tsl = slice(t * 512, (t + 1) * 512)
                for k in range(8):
                    P.dma(xb[:, k, :], self.xpark[k * 128:(k + 1) * 128, tsl], reads=[self.Bxpark[k][t]], writes=[Bxb[k]], eng="pool")
                for half in range(2):
                    slab, Bslab = load_slab(w_in, 3088 + half * 512, 512)
                    for kq in range(4):
                        k = half * 4 + kq
                        pb_, Bpb_ = ps[6 + pcnt[0] % 2], Bps[6 + pcnt[0] % 2]
                        pcnt[0] += 1
                        for kk in range(8):
                            P.mm(pb_[:], slab[:, kk, kq * 128:(kq + 1) * 128], xb[:, kk, :], kk == 0, kk == 7,
                                 reads=[Bslab, Bxb[kk]], writes=[Bpb_])
                        P.copy("act", uT[:, k, :], pb_[:], reads=[Bpb_], writes=[BuT[k]])
                for c in range(4):
                    csl = slice(c * 128, (c + 1) * 128)
                    for k in range(8):
                        i = it_ % 2
                        it_ += 1
                        pa, Bpa = ps[2 * i], Bps[2 * i]
                        pbm, Bpbm = ps[2 * i + 1], Bps[2 * i + 1]
                        for jj in range(4):
                            j = 4 * k + jj
                            P.mm(pa[:, jj * 128:(jj + 1) * 128], s5Bb[:, j, 0, :], uT[:, k, csl], True, True,
                                 reads=[Bs5Bb, BuT[k]], writes=[Bpa])
                            P.mm(pbm[:, jj * 128:(jj + 1) * 128], s5Bb[:, j, 1, :], uT[:, k, csl], True, True,
                                 reads=[Bs5Bb, BuT[k]], writes=[Bpbm])
                        P.copy("act", a32[i][:], V(pa[:], 4), reads=[Bpa], writes=[Ba32[i]])
                        P.copy("act", b32[i][:], V(pbm[:], 4), reads=[Bpbm], writes=[Bb32[i]])
                        ck = cosT[:, 4 * k:4 * k + 4, :]
                        sk = sinT[:, 4 * k:4 * k + 4, :]
                        P.tt("dve", t1[0][:], a32[i][:], ck, ALU.mult, reads=[Ba32[i], Btab], writes=[Bt1[0]])
                        P.tt("pool", t2[0][:], b32[i][:], sk, ALU.mult, reads=[Bb32[i], Btab], writes=[Bt2[0]])
                        P.tt("dve", mre[i][:], t1[0][:], t2[0][:], ALU.add, reads=[Bt1[0], Bt2[0]], writes=[Bmre[i]])
                        P.tt("pool", t2[0][:], b32[i][:], ck, ALU.mult, reads=[Bb32[i], Btab], writes=[Bt2[0]])
                        P.tt("dve", t1[0][:], a32[i][:], sk, ALU.mult, reads=[Ba32[i], Btab], writes=[Bt1[0]])
                        P.tt("pool", mim[i][:], t2[0][:], t1[0][:], ALU.subtract, reads=[Bt1[0], Bt2[0]], writes=[Bmim[i]])
                        for jj in range(4):
                            j = 4 * k + jj
                            rb = R[:, j:j + 1].broadcast_to([128, 128])
                            P.scan(wre[i][:, jj, :], rb, mre[i][:, jj, :], zc[:, j:j + 1], reads=[Bpr, Bmre[i], Bzc[k]], writes=[Bwre[i]])
                            P.scan(wim[i][:, jj, :], rb, mim[i][:, jj, :], zc[:, 32 + j:33 + j], reads=[Bpr, Bmim[i], Bzc[k]], writes=[Bwim[i]])
                        we_r, we_i = wre[i][:, :, 127], wim[i][:, :, 127]
                        rr, ri = RR[:, 4 * k:4 * k + 4], RI[:, 4 * k:4 * k + 4]
                        P.tt("pool", ct[:, 0, :], we_r, rr, ALU.mult, reads=[Bwre[i], Bpr], writes=[Bct])
                        P.tt("pool", ct[:, 1, :], we_i, ri, ALU.mult, reads=[Bwim[i], Bpr], writes=[Bct])
                        P.tt("pool", ct[:, 2, :], we_r, ri, ALU.mult, reads=[Bwre[i], Bpr], writes=[Bct])
                        P.tt("pool", ct[:, 3, :], we_i, rr, ALU.mult, reads=[Bwim[i], Bpr], writes=[Bct])
                        P.tt("pool", zc[:, 4 * k:4 * k + 4], ct[:, 0, :], ct[:, 1, :], ALU.subtract, reads=[Bct], writes=[Bzc[k]])
                        P.tt("pool", zc[:, 32 + 4 * k:36 + 4 * k], ct[:, 2, :], ct[:, 3, :], ALU.add, reads=[Bct], writes=[Bzc[k]])
                        if so:
                            continue
                        P.tt("dve", Pv[0][0][i][:], wre[i][:], ck, ALU.mult, reads=[Bwre[i], Btab], writes=[Pv[0][1][i]])
                        P.tt("pool", Pv[1][0][i][:], wim[i][:], sk, ALU.mult, reads=[Bwim[i], Btab], writes=[Pv[1][1][i]])
                        P.tt("dve", Pv[2][0][i][:], wre[i][:], sk, ALU.mult, reads=[Bwre[i], Btab], writes=[Pv[2][1][i]])
                        P.tt("pool", Pv[3][0][i][:], wim[i][:], ck, ALU.mult, reads=[Bwim[i], Btab], writes=[Pv[3][1][i]])
                        py, Bpy = ps[4 + k // 4], Bps[4 + k // 4]
                        ksl = slice((k % 4) * 128, (k % 4 + 1) * 128)
                        P.mm(py[:, ksl], diagD5[:, k, :], uT[:, k, csl], True, False, reads=[BdiagD5, BuT[k]], writes=[Bpy])
                        for jj in range(4):
                            j = 4 * k + jj
                            kw = {} if jj == 0 else {"tile_position": (0, 32 * jj)}
                            for v, cv in enumerate((0, 1, 2, 2)):
                                P.mm(py[32 * jj:32 * jj + 32, ksl], Ctab[:, j, cv, :], Pv[v][0][i][:, jj, :], False,
                                     jj == 3 and v == 3, reads=[BCtab, Pv[v][1][i]], writes=[Bpy], **kw)
                    if not so:
                        P.copy("act", y5T[:, 0:4, csl], V(ps[4][:], 4), reads=[Bps[4]], writes=By5[0:4])
                        P.copy("act", y5T[:, 4:8, csl], V(ps[5][:], 4), reads=[Bps[5]], writes=By5[4:8])
                for k in range(0 if so else 8):
                    yk = y5T[:, k, :]
                    P.tt("dve", gt[:], yk, yk, ALU.mult, reads=[By5[k]], writes=[Bgt])
                    P.ts("dve", gt[:], gt[:], 0.044715, 1.0, ALU.mult, ALU.add, reads=[Bgt], writes=[Bgt])
                    P.tt("dve", gt[:], gt[:], yk, ALU.mult, reads=[Bgt, By5[k]], writes=[Bgt])
                    P.act(gt[:], gt[:], AF.Tanh, scale=0.7978845608028654, reads=[Bgt], writes=[Bgt])
                    P.ts("dve", gt[:], gt[:], 1.0, 0.5, ALU.add, ALU.mult, reads=[Bgt], writes=[Bgt])
                    P.tt("dve", yk, gt[:], yk, ALU.mult, reads=[Bgt, By5[k]], writes=[By5[k]])
                    P.copy("act", gb[:, k, :], yk, reads=[By5[k]], writes=[Bgb[k]])
                for half in range(0 if so else 2):
                    slab, Bslab = load_slab(self.s5_w_glu, half * 512, 512)
                    for kq in range(4):
                        kk = half * 4 + kq
                        pg, Bpg = ps[6 + pcnt[0] % 2], Bps[6 + pcnt[0] % 2]
                        pcnt[0] += 1
                        for k in range(8):
                            P.mm(pg[:], slab[:, k, kq * 128:(kq + 1) * 128], gb[:, k, :], k == 0, k == 7,
                                 reads=[Bslab, Bgb[k]], writes=[Bpg])
                        P.act(sg[:], pg[:], AF.Sigmoid, bias=bglu(kk), reads=[Bpg, Bsmall], writes=[Bsg])
                        P.tt("dve", ycatB[:, kk, tsl], y5T[:, kk, :], sg[:], ALU.mult, reads=[By5[kk], Bsg], writes=[BycatB[kk][t]])
            if so:
                P.dma(self.st_ab.ap()[:, 1072:1136], zc[:], reads=Bzc, writes=[self.Bst_ab])
            P.barrier()

    def build(self):
        self.prologue()
        for seg in self.stages:
            kind = seg[0]
            if kind == "open":
                self.open_x32(self.xT if seg[1] == "in" else self.xpark)
            elif kind == "close":
                self.close_x32(self.out if seg[1] == "out" else self.xpark)
            elif kind == "ffn":
                self.ffn(seg[1], seg[2])
            elif kind == "ple":
                self.ple(seg[1])
            elif kind == "mix_cd":
                self.mixer_cd(**(seg[1] if len(seg) > 1 else {}))
            elif kind == "xchg_ab":
                self.exchange(self.st_ab, self.ga_ab, self.ini_ab, 1136, self.Bst_ab, self.Bini_ab)
            elif kind == "xchg_cd":
                self.exchange(self.st_cd, self.ga_cd, self.ini_cd, 768, self.Bst_cd, self.Bini_cd)
            elif kind == "mix_ab":
                self.mixer_ab(**(seg[1] if len(seg) > 1 else {}))
            elif kind == "copy_out":
                self.open_x32(self.xpark)
                self.close_x32(self.out)
        self.P.emit()
        self.st.close()
        return self.nc


FULL_STAGES = [("open", "in"), ("ffn", 0, 0), ("close", "park"),
               ("mix_ab", {"so": True}), ("xchg_ab",), ("mix_ab",),
               ("open", "park"), ("ffn", 0, 1), ("ple", 0), ("ffn", 1, 0), ("close", "park"),
               ("mix_cd", {"so": True}), ("xchg_cd",), ("mix_cd",),
               ("open", "park"), ("ffn", 1, 1), ("ple", 1), ("close", "out")]


def prep_inputs(inp, inits=None):
    f = lambda n: np.asarray(inp[n], np.float32)
    x, p = f("x"), f("p")
    g = f("ln_g").reshape(DEPTH, 3, 8, 128)
    b = f("ln_b").reshape(DEPTH, 3, 8, 128)
    lngb = np.stack([g, b], axis=2)
    lngb = np.ascontiguousarray(lngb.transpose(4, 0, 1, 2, 3).reshape(128, -1))
    shared = {"lngb": lngb}
    for n in ("ffn_w_gate", "ffn_w_up", "ffn_w_down", "ple_w_gate", "ple_w_proj"):
        shared[n] = np.ascontiguousarray(f(n))
    consts = np.zeros((128, 256), np.float32)
    consts[:, 0:128] = np.eye(128, dtype=np.float32)
    consts[:, 128:256] = np.triu(np.ones((128, 128), np.float32))
    shared["consts"] = consts
    shared["ab_w_in"] = np.ascontiguousarray(f("ab_w_in")[0])
    shared["ab_w_out"] = np.ascontiguousarray(f("ab_w_out")[0])
    shared["s5_w_glu"] = np.ascontiguousarray(f("s5_w_glu")[0])
    small = np.zeros((128, 112), np.float32)
    cw = f("ssd_conv_w")[0]
    small[:, 0:64] = cw.reshape(4, 16, 128).transpose(2, 1, 0).reshape(128, 64)
    small[:, 64:80] = f("ssd_conv_b")[0].reshape(16, 128).T
    small[:, 80:88] = np.repeat(f("ssd_d")[0], 64).reshape(8, 128).T
    small[:, 88:96] = f("ssd_norm_w")[0].reshape(8, 128).T
    small[:, 96:104] = f("s5_d")[0].reshape(8, 128).T
    small[:, 104:112] = f("s5_b_glu")[0].reshape(8, 128).T
    shared["ab_small"] = small
    shared["ssd16"] = np.ascontiguousarray(np.stack([f("ssd_dt_bias")[0], f("ssd_a_log")[0]], axis=1))
    def st_layout(a):
        return np.ascontiguousarray(a.reshape(32, 2, 64).transpose(1, 2, 0).reshape(128, 32))
    lam = np.stack([st_layout(f("s5_lambda_re")[0]), st_layout(f("s5_lambda_im")[0]),
                    st_layout(np.repeat(f("s5_log_dt")[0][:, None], 64, axis=1))], axis=1)
    shared["s5lam"] = np.ascontiguousarray(lam)
    Bre, Bim = f("s5_b_re")[0], f("s5_b_im")[0]
    s5B = np.zeros((128, 32, 2, 128), np.float32)
    Cre, Cim = f("s5_c_re")[0], f("s5_c_im")[0]
    s5C = np.zeros((128, 32, 2, 32), np.float32)
    for gg in range(64):
        j, e = gg // 2, gg % 2
        r0 = (gg % 8) * 16
        s5B[r0:r0 + 16, j, 0, e * 64:(e + 1) * 64] = Bre[gg].T
        s5B[r0:r0 + 16, j, 1, e * 64:(e + 1) * 64] = Bim[gg].T
        s5C[e * 64:(e + 1) * 64, j, 0, e * 16:(e + 1) * 16] = Cre[gg].T
        s5C[e * 64:(e + 1) * 64, j, 1, e * 16:(e + 1) * 16] = Cim[gg].T
    shared["s5B"] = s5B
    shared["s5C"] = s5C
    shared["cd_w_in"] = np.ascontiguousarray(f("cd_w_in")[0])
    shared["cd_w_out"] = np.ascontiguousarray(f("cd_w_out")[0])
    cs = np.zeros((128, 18), np.float32)
    lbl = f("hgrn_lb_logits")
    cs[:, 0:4] = lbl[0].reshape(4, 128).T
    cs[:, 4:8] = lbl[1].reshape(4, 128).T
    cs[:, 8:12] = f("hgrn_norm_w")[0].reshape(4, 128).T
    cs[:, 12:16] = f("gla_norm_w")[0].reshape(4, 128).T
    cs[:, 16:18] = f("gla_b_gate")[0].reshape(2, 128).T
    shared["cd_small"] = cs
    shared["gla_wup"] = np.ascontiguousarray(f("gla_w_gate_up")[0])
    maps = []
    big = [n for n in shared if shared[n].nbytes >= (1 << 20)]
    for c in range(8):
        bi, h = c // 2, c % 2
        m = dict(shared)
        for n in big:
            m[n] = np.concatenate([shared[n].reshape(-1), np.full(16, float(c), np.float32)])
        m["xT"] = np.ascontiguousarray(x[bi, h * T:(h + 1) * T, :].T)
        m["pT"] = np.ascontiguousarray(p[:, bi, h * T:(h + 1) * T, :].transpose(0, 2, 1))
        sel = np.zeros((128, 8), np.float32)
        if h == 1:
            sel[:, c - 1] = 1.0
        m["sel"] = sel
        maps.append(m)
    return maps


def run(inp, stages, cores=8, trace=False, inits=None, full=False):
    nc = K(stages).build()
    maps = prep_inputs(inp, inits)[:cores]
    if trace:
        res = run_bass_kernel_spmd(nc, maps, core_ids=list(range(cores)), trace=True)
        print("exec_time_ns", res.exec_time_ns)
    else:
        res = run_bass_kernel_spmd(nc, maps, core_ids=list(range(cores)))
    if full:
        return [{k: np.asarray(v) for k, v in r.items()} for r in res.results]
    return [np.asarray(r["outT"]) for r in res.results]


def kernel(**inputs):
    nc = K(FULL_STAGES).build()
    maps = prep_inputs(inputs)
    res = run_bass_kernel_spmd(nc, maps, core_ids=list(range(8))).results
    B = inputs["x"].shape[0]
    y = np.empty((B, 2 * T, D), np.float32)
    for c in range(8):
        y[c // 2, (c % 2) * T:(c % 2 + 1) * T, :] = np.asarray(res[c]["outT"]).T
    return y
```
